# Optimizing a Trainium2 kernel written in Bass

```python
import math
import jax, jax.numpy as jnp
from jax import lax
import numpy as np

D_MODEL = 1024
BATCH = 2
SEQ = 8192
DEPTH = 1

D_MIX = D_MODEL
ATTN_HEADS = 8
ATTN_KV_HEADS = 2
ATTN_HEAD_DIM = 64
ATTN_GROUP = ATTN_HEADS // ATTN_KV_HEADS
WINDOW = 128
BLOCK = 128
N_BUCKETS = 32
MAX_DISTANCE = 128
GDN_HEADS = 4
GDN_HEAD_DIM = 128
CONV_K = 5
CHUNK = 64
N_DIR = 2
D_FF = 4 * D_MODEL
EPS = 1e-6

ATTN_Q = ATTN_HEADS * ATTN_HEAD_DIM
ATTN_KV = ATTN_KV_HEADS * ATTN_HEAD_DIM
GDN_W = GDN_HEADS * GDN_HEAD_DIM
D_IN = ATTN_Q + 2 * ATTN_KV + 4 * GDN_W + 2 * N_DIR * GDN_HEADS
SPLIT_POINTS = (ATTN_Q,
                ATTN_Q + ATTN_KV,
                ATTN_Q + 2 * ATTN_KV,
                ATTN_Q + 2 * ATTN_KV + 3 * GDN_W,
                ATTN_Q + 2 * ATTN_KV + 4 * GDN_W,
                ATTN_Q + 2 * ATTN_KV + 4 * GDN_W + N_DIR * GDN_HEADS)

kernel_name = "hymba_swa_gdn_relu2_encoder"


def _rmsnorm(x, w):
    x32 = x.astype(jnp.float32)
    y = x32 * lax.rsqrt(jnp.mean(x32 * x32, axis=-1, keepdims=True) + EPS)
    return (y * w.astype(jnp.float32)).astype(x.dtype)


def _l2norm(t):
    return t * lax.rsqrt(jnp.sum(t * t, axis=-1, keepdims=True) + EPS)


def _t5_buckets(rel):
    nb = N_BUCKETS // 2
    max_exact = nb // 2
    base = jnp.where(rel > 0, nb, 0)
    n = jnp.abs(rel)
    log_ratio = jnp.log(jnp.maximum(n, 1).astype(jnp.float32) / max_exact) / math.log(MAX_DISTANCE / max_exact)
    large = jnp.minimum(max_exact + (log_ratio * (nb - max_exact)).astype(jnp.int32), nb - 1)
    return base + jnp.where(n < max_exact, n, large)


def _band_bias(rel_bias):
    rel = (np.arange(3 * BLOCK)[None, :] - BLOCK) - np.arange(BLOCK)[:, None]
    bucket = _t5_buckets(jnp.asarray(rel, dtype=jnp.int32))
    bias = jnp.transpose(rel_bias[bucket], (2, 0, 1)).astype(jnp.float32)
    return bias, np.abs(rel) <= WINDOW


def _band(t):
    B, S = t.shape[:2]
    nb = S // BLOCK
    tp = jnp.pad(t, ((0, 0), (BLOCK, BLOCK), (0, 0), (0, 0))).reshape(B, nb + 2, BLOCK, t.shape[2], t.shape[3])
    return jnp.concatenate([tp[:, :-2], tp[:, 1:-1], tp[:, 2:]], axis=2)


def _windowed_gqa(q, k, v, bias, in_window, sink):
    B, S = q.shape[:2]
    nb = S // BLOCK
    qb = q.reshape(B, nb, BLOCK, ATTN_KV_HEADS, ATTN_GROUP, ATTN_HEAD_DIM)
    kb, vb = _band(k), _band(v)
    scores = jnp.einsum("bnqkgd,bnskd->bnkgqs", qb, kb).astype(jnp.float32) * (ATTN_HEAD_DIM ** -0.5)
    scores = scores + bias.reshape(ATTN_KV_HEADS, ATTN_GROUP, BLOCK, 3 * BLOCK)[None, None]
    key_pos = np.arange(nb)[:, None] * BLOCK - BLOCK + np.arange(3 * BLOCK)[None, :]
    valid = in_window[None] & ((key_pos >= 0) & (key_pos < S))[:, None, :]
    scores = jnp.where(valid[None, :, None, None], scores, -1e30)
    sink_col = jnp.broadcast_to(
        sink.astype(jnp.float32).reshape(ATTN_KV_HEADS, ATTN_GROUP)[None, None, :, :, None, None],
        scores.shape[:-1] + (1,))
    probs = jax.nn.softmax(jnp.concatenate([scores, sink_col], axis=-1), axis=-1)[..., :-1]
    out = jnp.einsum("bnkgqs,bnskd->bnqkgd", probs.astype(v.dtype), vb)
    return out.reshape(B, S, ATTN_Q)


def _centred_conv(x, w):
    C = x.shape[-1]
    return lax.conv_general_dilated(
        x, w[:, None, :].astype(x.dtype), window_strides=(1,),
        padding=((CONV_K // 2, CONV_K // 2),),
        dimension_numbers=("NWC", "WIO", "NWC"), feature_group_count=C)


def _chunk_gated_delta_rule(q, k, v, g, beta):
    B, S, H, DK = q.shape
    DV = v.shape[-1]
    NC = S // CHUNK

    def to_chunks(t):
        return jnp.moveaxis(t.reshape((B, NC, CHUNK) + t.shape[2:]), 2, 3)

    q = to_chunks(q * (DK ** -0.5))
    k, v, g, beta = to_chunks(k), to_chunks(v), to_chunks(g), to_chunks(beta)
    gc = jnp.cumsum(g, axis=-1)
    causal = np.tril(np.ones((CHUNK, CHUNK), dtype=bool))
    strict = np.tril(np.ones((CHUNK, CHUNK), dtype=bool), -1)
    decay = jnp.exp(jnp.where(causal, gc[..., :, None] - gc[..., None, :], -jnp.inf))
    kk = jnp.einsum("bnhcd,bnhsd->bnhcs", k, k)
    a_mat = jnp.where(strict, beta[..., :, None] * kk * decay, 0.0)
    eye = jnp.eye(CHUNK, dtype=jnp.float32)
    t_mat = lax.linalg.triangular_solve(eye + a_mat, jnp.broadcast_to(eye, a_mat.shape),
                                        left_side=True, lower=True)
    w_c = jnp.einsum("bnhcs,bnhsd->bnhcd", t_mat, beta[..., None] * k * jnp.exp(gc)[..., None])
    u_c = jnp.einsum("bnhcs,bnhse->bnhce", t_mat, beta[..., None] * v)
    qk = jnp.einsum("bnhcd,bnhsd->bnhcs", q, k) * decay
    q_dec = q * jnp.exp(gc)[..., None]
    g_last = gc[..., -1]
    k_dec = k * jnp.exp(g_last[..., None] - gc)[..., None]

    def step(state, xs):
        q_c, k_c, u, w, qk_c, gl = xs
        v_new = u - jnp.einsum("bhcd,bhde->bhce", w, state)
        o = jnp.einsum("bhcd,bhde->bhce", q_c, state) + jnp.einsum("bhcs,bhse->bhce", qk_c, v_new)
        state = state * jnp.exp(gl)[..., None, None] + jnp.einsum("bhcd,bhce->bhde", k_c, v_new)
        return state, o

    xs = tuple(jnp.moveaxis(t, 1, 0) for t in (q_dec, k_dec, u_c, w_c, qk, g_last))
    state0 = jnp.zeros((B, H, DK, DV), dtype=jnp.float32)
    _, o = lax.scan(step, state0, xs)
    return jnp.transpose(o, (1, 0, 3, 2, 4)).reshape(B, S, H, DV)


def _gdn_mixer(qkv, z, a, b, conv_w, a_log, dt_bias, norm_w):
    B, S, _ = qkv.shape
    f32 = jnp.float32
    qkv = jax.nn.silu(_centred_conv(qkv, conv_w)).astype(f32)
    q, k, v = jnp.split(qkv, 3, axis=-1)
    q = _l2norm(q.reshape(B, S, GDN_HEADS, GDN_HEAD_DIM))
    k = _l2norm(k.reshape(B, S, GDN_HEADS, GDN_HEAD_DIM))
    v = v.reshape(B, S, GDN_HEADS, GDN_HEAD_DIM)
    a = a.astype(f32).reshape(B, S, N_DIR, GDN_HEADS)
    b = b.astype(f32).reshape(B, S, N_DIR, GDN_HEADS)
    g = -jnp.exp(a_log.astype(f32)) * jax.nn.softplus(a + dt_bias.astype(f32))
    beta = jax.nn.sigmoid(b)
    o_fwd = _chunk_gated_delta_rule(q, k, v, g[:, :, 0], beta[:, :, 0])
    flip = lambda t: jnp.flip(t, axis=1)
    o_bwd = flip(_chunk_gated_delta_rule(flip(q), flip(k), flip(v), flip(g[:, :, 1]), flip(beta[:, :, 1])))
    o = _rmsnorm(o_fwd + o_bwd, norm_w) * jax.nn.silu(z.astype(f32).reshape(B, S, GDN_HEADS, GDN_HEAD_DIM))
    return o.reshape(B, S, GDN_W).astype(z.dtype)


def setup_inputs(seed: int = 0) -> dict:
    key = jax.random.key(seed)
    ks = jax.random.split(key, 16)
    nrm = lambda k, shape, scale: jax.random.normal(k, shape, dtype=jnp.float32) * scale
    gain = lambda k, shape: 1.0 + 0.02 * jax.random.normal(k, shape, dtype=jnp.float32)
    dt = jnp.exp(jax.random.uniform(ks[7], (DEPTH, N_DIR, GDN_HEADS), minval=math.log(1e-3), maxval=math.log(1e-1)))
    return {
        "x": nrm(ks[0], (BATCH, SEQ, D_MODEL), 1.0),
        "norm_mix_w": gain(ks[1], (DEPTH, D_MODEL)),
        "w_in": nrm(ks[2], (DEPTH, D_MODEL, D_IN), D_MODEL ** -0.5),
        "rel_bias": nrm(ks[3], (N_BUCKETS, ATTN_HEADS), 0.5),
        "attn_sink": nrm(ks[4], (DEPTH, ATTN_HEADS), 1.0),
        "conv_w": nrm(ks[5], (DEPTH, CONV_K, 3 * GDN_W), CONV_K ** -0.5),
        "gdn_a_log": jnp.log(jax.random.uniform(ks[6], (DEPTH, N_DIR, GDN_HEADS), minval=1.0, maxval=16.0)),
        "gdn_dt_bias": dt + jnp.log(-jnp.expm1(-dt)),
        "gdn_norm_w": gain(ks[8], (DEPTH, GDN_HEAD_DIM)),
        "w_out": nrm(ks[9], (DEPTH, D_MIX, D_MODEL), D_MIX ** -0.5),
        "norm_ffn_w": gain(ks[10], (DEPTH, D_MODEL)),
        "w_ffn_in": nrm(ks[11], (DEPTH, D_MODEL, D_FF), D_MODEL ** -0.5),
        "w_ffn_out": nrm(ks[12], (DEPTH, D_FF, D_MODEL), D_FF ** -0.5),
        "norm_final_w": gain(ks[13], (D_MODEL,)),
    }


def reference(x, norm_mix_w, w_in, rel_bias, attn_sink, conv_w, gdn_a_log, gdn_dt_bias,
              gdn_norm_w, w_out, norm_ffn_w, w_ffn_in, w_ffn_out, norm_final_w):
    B, S, _ = x.shape
    band_bias, in_window = _band_bias(rel_bias)
    h = x
    for l in range(DEPTH):
        xn = _rmsnorm(h, norm_mix_w[l])
        proj = xn @ w_in[l]
        q_a, k_a, v_a, qkv_g, z_g, a_g, b_g = jnp.split(proj, SPLIT_POINTS, axis=-1)
        attn = _windowed_gqa(q_a.reshape(B, S, ATTN_HEADS, ATTN_HEAD_DIM),
                             k_a.reshape(B, S, ATTN_KV_HEADS, ATTN_HEAD_DIM),
                             v_a.reshape(B, S, ATTN_KV_HEADS, ATTN_HEAD_DIM),
                             band_bias, in_window, attn_sink[l])
        gdn = _gdn_mixer(qkv_g, z_g, a_g, b_g, conv_w[l], gdn_a_log[l], gdn_dt_bias[l], gdn_norm_w[l])
        h = h + jnp.concatenate([attn, gdn], axis=-1) @ w_out[l]
        hn = _rmsnorm(h, norm_ffn_w[l])
        h = h + jnp.square(jax.nn.relu(hn @ w_ffn_in[l])) @ w_ffn_out[l]
    return _rmsnorm(h, norm_final_w)
```

```python
import contextlib
import numpy as np
import concourse.bass as bass
import concourse.mybir as mybir
from concourse.bass_utils import run_bass_kernel_spmd

F32 = mybir.dt.float32
BF16 = mybir.dt.bfloat16
ALU = mybir.AluOpType
AF = mybir.ActivationFunctionType

SEQ = 8192
D = 1024
DFF = 4096
D_IN = 2832
EPS = 1e-6
NSLOT = 10
WG = 1028
WA = 1280
SAME_ENGINE_SYNC = True
DEBUG_MIX = False
DEBUG_TAPS = False
RUN_GDN = True
RUN_FFN = True
RUN_ATT = True
GDN_SLOTS = None
GDN_STAGE = 99
OVERLAP = True


class Buf:
    __slots__ = ("ap", "w", "r", "name")

    def __init__(self, ap, name=""):
        self.ap = ap
        self.w = None
        self.r = {}
        self.name = name

    def __getitem__(self, k):
        return self.ap[k]


class Prog:
    ENG = ("pe", "act", "dve", "pool", "sp")

    def __init__(self, nc, stack):
        self.nc = nc
        self.stack = stack
        self.sem = {k: stack.enter_context(nc.semaphore("sem_" + k)) for k in self.ENG}
        self.cnt = {k: 0 for k in self.ENG}
        self.ops = {k: [] for k in self.ENG}
        self.waited = {k: {} for k in self.ENG}
        self.pend = {k: ([], []) for k in self.ENG}
        self.dsem = {}

    def _waits(self, eng, r, w, force=False):
        need = {}

        def add(ev, raw=False):
            if ev is None:
                return
            s, sem, val = ev
            if s == "c_" + eng and (not SAME_ENGINE_SYNC or eng in ("pe", "sp") or not (raw or force)):
                return
            if need.get(s, (None, 0))[1] < val:
                need[s] = (sem, val)

        for b in r:
            add(b.w, raw=True)
        for b in w:
            add(b.w)
            for ev in b.r.values():
                add(ev)
        for s, (sem, val) in need.items():
            if self.waited[eng].get(s, 0) >= val:
                continue
            self.waited[eng][s] = val
            self.ops[eng].append(("wait", sem, val))

    def _record(self, ev, r, w):
        for b in r:
            b.r[ev[0]] = ev
        for b in w:
            b.w = ev
            b.r = {}

    def op(self, eng, fn, r=(), w=(), sig=True, force=False):
        self._waits(eng, r, w, force)
        if not sig:
            self.pend[eng][0].extend(r)
            self.pend[eng][1].extend(w)
            self.ops[eng].append(("op", fn, None, 0))
            return
        self.cnt[eng] += 1
        ev = ("c_" + eng, self.sem[eng], self.cnt[eng])
        self.ops[eng].append(("op", fn, self.sem[eng], 1))
        pr, pw = self.pend[eng]
        self._record(ev, list(r) + pr, list(w) + pw)
        self.pend[eng] = ([], [])

    def dma(self, out_ap, in_ap, r=(), w=(), key="d", q="sp"):
        self._waits(q, r, w)
        if key not in self.dsem:
            self.dsem[key] = [self.stack.enter_context(self.nc.semaphore("dsem_" + key)), 0]
        ent = self.dsem[key]
        ent[1] += 16
        ev = ("d_" + key, ent[0], ent[1])
        self.ops[q].append(("op", lambda e: e.dma_start(out=out_ap, in_=in_ap), ent[0], 16))
        self._record(ev, r, w)
        return ev

    def barrier(self):
        evs = [("c_" + k, self.sem[k], self.cnt[k]) for k in self.ENG if self.cnt[k] > 0]
        evs += [("d_" + k, v[0], v[1]) for k, v in self.dsem.items()]
        for eng in self.ENG:
            assert not self.pend[eng][0] and not self.pend[eng][1]
            for s, sem, val in evs:
                if s == "c_" + eng or self.waited[eng].get(s, 0) >= val:
                    continue
                self.waited[eng][s] = val
                self.ops[eng].append(("wait", sem, val))

    def final_wait(self, evs, q="sp"):
        for s, sem, val in evs:
            self.ops[q].append(("wait", sem, val))

    def replay(self, eng, e):
        for it in self.ops[eng]:
            if it[0] == "wait":
                e.wait_ge(it[1], it[2])
            else:
                ins = it[1](e)
                if it[2] is not None:
                    ins.then_inc(it[2], it[3])


def bc_last(ap2, n):
    return ap2.unsqueeze(2).to_broadcast([ap2.shape[0], ap2.shape[1], n])


def bc_mid(ap2, n):
    return ap2.unsqueeze(1).to_broadcast([ap2.shape[0], n, ap2.shape[1]])


def build_program():
    nc = bass.Bass("TRN2", target_bir_lowering=False)
    di = lambda name, shape: nc.dram_tensor(name, shape, F32, kind="ExternalInput").ap()
    xg = di("xg", [NSLOT, D, WG])
    xa = di("xa", [2, D, WA])
    w_in = di("w_in", [128, 23, 8, 128])
    wk_dup = di("wk_dup", [128, 8, 2, 128])
    w_out = di("w_out", [128, 8, 8, 128])
    w1 = di("w1", [128, 32, 8, 128])
    w2 = di("w2", [128, 8, 32, 128])
    nrm = di("nrm", [128, 3, 8])
    cst = di("cst", [128, 8, 128])
    biasT = di("biasT", [128, 8, 3, 128])
    kbias = di("kbias", [128, 2, 10])
    flg = di("flg", [128, 24])
    convw = di("convw", [128, 12, 5])
    convr = di("convr", [128, 12, 5])
    gpar = di("gpar", [128, 2, 8])
    gnw = di("gnw", [128, 1])
    sink = di("sink", [128, 4])
    outT = nc.dram_tensor("outT", [D, 2048], F32, kind="ExternalOutput").ap()
    dbg = nc.dram_tensor("dbg", [D, 2048], F32, kind="ExternalOutput").ap() if (DEBUG_MIX or DEBUG_TAPS) else None
    obw = nc.dram_tensor("obw", [16, 128, 512], BF16).ap()
    mixs = nc.dram_tensor("mixs", [8, 128, 2048], BF16).ap()

    with contextlib.ExitStack() as st:
        P = Prog(nc, st)

        cur = [st]
        nid = [0]

        def sb(name, shape, dt=F32):
            nid[0] += 1
            t = cur[0].enter_context(nc.sbuf_tensor("%s_%d" % (name, nid[0]), shape, dt))
            return t

        def sbuf(name, shape, dt=F32):
            return Buf(sb(name, shape, dt)[:], name)

        pst = st.enter_context(nc.psum_tensor("ps", [128, 8, 512], F32))
        PS = [Buf(pst[:, i, :], "ps%d" % i) for i in range(8)]
        gp_state = [0]
        GEN = [4, 5, 6, 7]

        def gps():
            gp_state[0] = (gp_state[0] + 1) % len(GEN)
            return PS[GEN[gp_state[0]]]

        cst_f = sbuf("cst_f", [128, 8, 128])
        cst_b = sbuf("cst_b", [128, 8, 128], BF16)
        P.dma(cst_f[:], cst, w=[cst_f], key="c0")
        P.op("dve", lambda e: e.tensor_copy(cst_b[:], cst_f[:]), r=[cst_f], w=[cst_b])
        ID_F, J_F, U_F, ONE_F = cst_f[:, 0, :], cst_f[:, 1, :], cst_f[:, 2, :], cst_f[:, 6, :]
        ID_B, ONE_B = cst_b[:, 0, :], cst_b[:, 6, :]
        LS_F, LD_F, LO_F = cst_f[:, 3, :], cst_f[:, 4, :], cst_f[:, 5, :]
        nrm_s = sbuf("nrm_s", [128, 3, 8])
        flg_s = sbuf("flg_s", [128, 24])
        cw_s = sbuf("cw_s", [128, 12, 5])
        cr_s = sbuf("cr_s", [128, 12, 5])
        cdiff = sbuf("cdiff", [128, 12, 5])
        gpar_s = sbuf("gpar_s", [128, 2, 8])
        gnw_s = sbuf("gnw_s", [128, 1])
        sink_s = sbuf("sink_s", [128, 4])
        kb_s = sbuf("kb_s", [128, 2, 10])
        small = [nrm_s, flg_s, cw_s, cr_s, gpar_s, gnw_s, sink_s, kb_s]
        for i, (b_, src) in enumerate(zip(small, [nrm, flg, convw, convr, gpar, gnw, sink, kbias])):
            P.dma(b_[:], src, w=[b_], key="c%d" % (i + 1))
        P.op("dve", lambda e: e.tensor_tensor(cdiff[:], cw_s[:], cr_s[:], ALU.subtract), r=[cw_s, cr_s], w=[cdiff])
        nA = sbuf("nA", [128, 8])
        P.op("act", lambda e: e.activation(nA[:], gpar_s[:, 0, :], AF.Exp), r=[gpar_s], w=[nA])
        P.op("dve", lambda e: e.tensor_scalar(nA[:], nA[:], -1.0, None, ALU.mult), r=[nA], w=[nA])
        nAd = sbuf("nAd", [128, 4])
        dtd = sbuf("dtd", [128, 4])
        P.op("dve", lambda e: e.tensor_tensor(nAd[:], nA[:, 0:4], nA[:, 4:8], ALU.subtract), r=[nA], w=[nAd])
        P.op("dve", lambda e: e.tensor_tensor(dtd[:], gpar_s[:, 1, 0:4], gpar_s[:, 1, 4:8], ALU.subtract), r=[gpar_s], w=[dtd])
        esink = sbuf("esink", [128, 4])
        eps_t = sbuf("eps_t", [128, 1])
        P.op("pool", lambda e: e.memset(eps_t[:], EPS), w=[eps_t])
        one_t = sbuf("one_t", [128, 1])
        P.op("pool", lambda e: e.memset(one_t[:], 1.0), w=[one_t])
        P.op("act", lambda e: e.activation(esink[:], sink_s[:], AF.Exp), r=[sink_s], w=[esink])

        WB = {}

        def alloc_wstage(nst, with_tmp):
            WB["wst"] = [sbuf("wst%d" % i, [128, 8, 128]) for i in range(nst)]
            WB["i"] = 0
            if with_tmp:
                WB["wtmp"] = [sbuf("wtmp%d" % i, [128, 8, 128], BF16) for i in range(2)]
                WB["ti"] = 0

        def load_w(dst_buf, dst_ap, src_ap, nrow):
            WB["i"] = (WB["i"] + 1) % len(WB["wst"])
            s_ = WB["wst"][WB["i"]]
            P.dma(s_[:], src_ap, w=[s_], key="wst%d" % WB["i"])
            if nrow is None:
                P.op("act", lambda e: e.activation(dst_ap, s_[:], AF.Copy), r=[s_], w=[dst_buf])
            else:
                P.op("dve", lambda e: e.tensor_tensor(dst_ap, s_[:], bc_last(nrm_s[:, nrow, :], 128), ALU.mult),
                     r=[s_, nrm_s], w=[dst_buf])

        def stream_w(c0, dup=None):
            WB["ti"] ^= 1
            t_ = WB["wtmp"][WB["ti"]]
            src = w_in[:, c0 // 128, :, :] if dup is None else wk_dup[:, :, dup, :]
            load_w(t_, t_[:], src, 0)
            return t_

        B = {}
        mixs_buf = Buf(None, "mixs")

        def rsqrt_op(dst_ap, src_ap, scale, eps, r, w):
            P.op("act", lambda e: e.activation(dst_ap, src_ap, AF.Ln, bias=eps_t[:, 0:1], scale=scale), r=list(r) + [eps_t], w=w)
            P.op("act", lambda e: e.activation(dst_ap, dst_ap, AF.Exp, scale=-0.5), r=w, w=w)

        def ntiles(Wd):
            t, o = [], 0
            while o < Wd:
                n = min(512, Wd - o)
                t.append((o, n))
                o += n
            return t

        def load_slot(src, Wd):
            xT, stg, sqt, rrow = B["xT"], B["stg"], B["sqt"], B["rrow"]
            tl = ntiles(Wd)
            banks = [gps() for _ in tl]
            for k in range(8):
                s_ = stg[k % 2]
                q_ = sqt[k % 2]
                P.dma(s_[:, 0:Wd], src[k * 128:(k + 1) * 128, :], w=[s_], key="stg%d" % (k % 2))
                P.op("dve", lambda e, s_=s_, k=k: e.tensor_copy(xT[:, k, 0:Wd], s_[:, 0:Wd]), r=[s_], w=[xT])
                P.op("act", lambda e, s_=s_, q_=q_: e.activation(q_[:, 0:Wd], s_[:, 0:Wd], AF.Square), r=[s_], w=[q_])
                for ti_, ((o, n), bk) in enumerate(zip(tl, banks)):
                    P.op("pe", lambda e, bk=bk, q_=q_, o=o, n=n, k=k: e.matmul(bk[:, 0:n], ONE_B, q_[:, o:o + n], start=(k == 0), stop=(k == 7)),
                         r=[q_, cst_b], w=[bk], sig=(k == 7 or ti_ == len(tl) - 1))
            for (o, n), bk in zip(tl, banks):
                rsqrt_op(rrow[:, o:o + n], bk[:, 0:n], 1.0 / D, EPS, [bk], [rrow])

        def proj_fm(wt_buf, wt_ap, Wd, evac):
            xT = B["xT"]
            for (o, n) in ntiles(Wd):
                bk = gps()
                for k in range(8):
                    P.op("pe", lambda e, bk=bk, k=k, o=o, n=n: e.matmul(bk[:, 0:n], wt_ap[:, k, :], xT[:, k, o:o + n], start=(k == 0), stop=(k == 7)),
                         r=[wt_buf, xT], w=[bk], sig=(k == 7))
                evac(bk, o, n)


        def alloc_slot(Wd):
            B["xT"] = sbuf("xT", [128, 8, Wd], BF16)
            B["stg"] = [sbuf("stg%d" % i, [128, Wd]) for i in range(2)]
            B["sqt"] = [sbuf("sqt%d" % i, [128, Wd], BF16) for i in range(2)]
            B["rrow"] = sbuf("rrow", [128, Wd])
            return B["xT"], B["rrow"]

        with contextlib.ExitStack() as s_att:
            cur[0] = s_att
            GEN[:] = [0, 1, 2, 3, 4, 5, 6, 7]
            xTa, rrowa = alloc_slot(WA)
            attnT = sbuf("attnT", [128, 4, 2048], BF16)
            alloc_wstage(2, True)
            QT = [sbuf("QT%d" % i, [128, 2, 1024], BF16) for i in range(4)]
            KT = [sbuf("KT%d" % i, [128, WA], BF16) for i in range(2)]
            Vp = sbuf("Vp", [128, 10, 2, 2, 128], BF16)
            bT_f = sbuf("bT_f", [128, 8, 3, 128])
            bT = sbuf("bT", [128, 8, 3, 128], BF16)
            P.dma(bT_f[:], biasT, w=[bT_f], key="bT")
            P.op("pool", lambda e: e.tensor_copy(bT[:], bT_f[:]), r=[bT_f], w=[bT])
            wv_b = sbuf("wv_b", [128, 8, 128], BF16)
            load_w(wv_b, wv_b[:], w_in[:, 5, :, :], 0)
            PTs = [sbuf("PT%d" % i, [128, 2, 3, 128], BF16) for i in range(2)]
            rtk10 = sbuf("rtk10", [128, 10])
            rden = sbuf("rden", [128, 128])
            P.op("pool", lambda e: e.memset(Vp[:], 0.0), w=[Vp])
            onesp_b = sbuf("onesp", [128, 2, 128], BF16)
            onesp = onesp_b
            P.op("pool", lambda e: e.memset(onesp_b[:], 0.0), w=[onesp_b])
            P.op("pool", lambda e: e.memset(onesp_b[:, 0, 0:64], 1.0), w=[onesp_b], force=True)
            P.op("pool", lambda e: e.memset(onesp_b[:, 1, 64:128], 1.0), w=[onesp_b], force=True)
            hm0, hm1 = cst_f[:, 7, 0:1], cst_f[:, 7, 1:2]
            def att_half(hs):
                load_slot(xa[hs], WA)
                for j in range(4):
                    wt = stream_w(j * 128)
                    def ev_q(bk, o, n, j=j):
                        lo_, hi_ = max(o, 128), min(o + n, 1152)
                        if hi_ <= lo_:
                            return
                        for e_, hm in ((0, hm0), (1, hm1)):
                            P.op("dve", lambda e, e_=e_, hm=hm: e.scalar_tensor_tensor(QT[j][:, e_, lo_ - 128:hi_ - 128], bk[:, lo_ - o:hi_ - o], hm, rrowa[:, lo_:hi_], ALU.mult, ALU.mult),
                                 r=[bk, cst_f, rrowa], w=[QT[j]])
                    proj_fm(wt, wt[:], WA, ev_q)
                for kv in range(2):
                    wt = stream_w(0, dup=kv)
                    proj_fm(wt, wt[:], WA,
                            lambda bk, o, n, kv=kv: P.op("dve", lambda e: e.scalar_tensor_tensor(KT[kv][:, o:o + n], bk[:, 0:n], 0.125, rrowa[:, o:o + n], ALU.mult, ALU.mult), r=[bk, rrowa], w=[KT[kv]]))
                brt = gps()
                for kb in range(10):
                    P.op("pe", lambda e, kb=kb: e.matmul(brt[:, kb:kb + 1], rrowa[0:1, kb * 128:(kb + 1) * 128], cst_f[0:1, 6, 0:1], start=True, stop=True), r=[rrowa, cst_f], w=[brt], sig=(kb == 9))
                P.op("act", lambda e: e.activation(rtk10[:], brt[:, 0:10], AF.Copy), r=[brt], w=[rtk10])
                for kb in range(10):
                    bk = gps()
                    for k in range(8):
                        P.op("pe", lambda e, bk=bk, kb=kb, k=k: e.matmul(bk[:, 0:128], xTa[:, k, kb * 128:(kb + 1) * 128], wv_b[:, k, :], start=(k == 0), stop=(k == 7)), r=[xTa, wv_b], w=[bk], sig=(k == 7))
                    for kv in range(2):
                        for e_ in range(2):
                            P.op("dve", lambda e, bk=bk, kb=kb, kv=kv, e_=e_: e.tensor_scalar(Vp[:, kb, kv, e_, e_ * 64:(e_ + 1) * 64], bk[:, kv * 64:(kv + 1) * 64], rtk10[:, kb:kb + 1], None, ALU.mult),
                                 r=[bk, rtk10], w=[Vp])
                def att_st(qb, j):
                    kv = j // 2
                    bse = [gps(), gps()]
                    for e_ in range(2):
                        head = 2 * j + e_
                        for kbr in range(3):
                            kb = qb + kbr
                            P.op("pe", lambda e, e_=e_, kbr=kbr, kb=kb: e.matmul(bse[e_][:, kbr * 128:(kbr + 1) * 128], KT[kv][:, kb * 128:(kb + 1) * 128], QT[j][:, e_, qb * 128:(qb + 1) * 128], start=True, stop=False),
                                 r=[KT[kv], QT[j]], w=[bse[e_]], sig=False)
                            P.op("pe", lambda e, e_=e_, kbr=kbr, head=head: e.matmul(bse[e_][:, kbr * 128:(kbr + 1) * 128], ID_B, bT[:, head, kbr, :], start=False, stop=True),
                                 r=[cst_b, bT], w=[bse[e_]], sig=(kbr == 2))
                    return bse

                def att_pv(qb, j, bse, PT):
                    kv = j // 2
                    for e_ in range(2):
                        if 1 <= qb <= 6:
                            P.op("act", lambda e, e_=e_: e.activation(PT[:, e_, :, :].rearrange("p k n -> p (k n)"), bse[e_][:, 0:384], AF.Exp), r=[bse[e_]], w=[PT])
                        else:
                            for kbr in range(3):
                                kb = qb + kbr
                                P.op("act", lambda e, e_=e_, kbr=kbr, kb=kb: e.activation(PT[:, e_, kbr, :], bse[e_][:, kbr * 128:(kbr + 1) * 128], AF.Exp, bias=kb_s[:, hs, kb:kb + 1]),
                                     r=[bse[e_], kb_s], w=[PT])
                    bo = gps()
                    n_mm = 0
                    for e_ in range(2):
                        for kbr in range(3):
                            kb = qb + kbr
                            P.op("pe", lambda e, e_=e_, kbr=kbr, kb=kb, n_mm=n_mm: e.matmul(bo[:, 0:128], Vp[:, kb, kv, e_, :], PT[:, e_, kbr, :], start=(n_mm == 0), stop=(n_mm == 5)),
                                 r=[Vp, PT], w=[bo], sig=False)
                            n_mm += 1
                    n_mm = 0
                    for e_ in range(2):
                        for kbr in range(3):
                            P.op("pe", lambda e, e_=e_, kbr=kbr, n_mm=n_mm: e.matmul(bo[:, 128:256], onesp[:, e_, :], PT[:, e_, kbr, :], start=(n_mm == 0), stop=(n_mm == 5)),
                                 r=[onesp_b, PT], w=[bo], sig=(n_mm == 5))
                            n_mm += 1
                    P.op("dve", lambda e, bo=bo, j=j: e.tensor_scalar(rden[:], bo[:, 128:256], esink[:, j:j + 1], None, ALU.add), r=[bo, esink], w=[rden])
                    P.op("dve", lambda e: e.reciprocal(rden[:], rden[:]), r=[rden], w=[rden])
                    qc = slice(hs * 1024 + qb * 128, hs * 1024 + (qb + 1) * 128)
                    P.op("dve", lambda e, bo=bo, j=j, qc=qc: e.tensor_tensor(attnT[:, j, qc], bo[:, 0:128], rden[:], ALU.mult), r=[bo, rden], w=[attnT])

                blocks = [(qb_, j_) for qb_ in range(8) for j_ in range(4)]
                bse_next = att_st(*blocks[0])
                for bi_, (qb_, j_) in enumerate(blocks):
                    bse_cur = bse_next
                    if bi_ + 1 < len(blocks):
                        bse_next = att_st(*blocks[bi_ + 1])
                    att_pv(qb_, j_, bse_cur, PTs[bi_ % 2])

            if RUN_ATT:
                att_half(0)
                att_half(1)
            for j_ in range(4):
                P.dma(mixs[j_], attnT[:, j_, :], r=[attnT], w=[mixs_buf], key="mixw")
        cur[0] = st
        P.barrier()
        GEN[:] = [4, 5, 6, 7]
        gp_state[0] = 0
        with contextlib.ExitStack() as s_gdn:
            cur[0] = s_gdn
            xT, rrow = alloc_slot(WG)
            alloc_wstage(1, True)
            wkv = sbuf("wkv", [128, 8, 8, 128], BF16)
            for g in range(8):
                c0 = 768 + 512 + g * 128
                load_w(wkv, wkv[:, g, :, :], w_in[:, c0 // 128, :, :], 0)
            wab_f = sbuf("wab_f", [128, 8, 16])
            wab = sbuf("wab", [128, 8, 16], BF16)
            P.dma(wab_f[:], w_in[:, 22, :, 0:16], w=[wab_f], key="wab")
            P.op("pool", lambda e: e.tensor_tensor(wab[:], wab_f[:], bc_last(nrm_s[:, 0, :], 16), ALU.mult), r=[wab_f, nrm_s], w=[wab])
            pre = [sbuf("pre%d" % i, [128, 516], BF16) for i in range(4)]
            post = [[sbuf("post%d_%d" % (g, hf), [128, 512], BF16) for hf in range(2)] for g in range(12)]
            Dg = [sbuf("Dg%d" % g, [128, 5, 128], BF16) for g in range(12)]
            for g_ in range(12):
                for j_ in range(5):
                    P.op("dve", lambda e, g_=g_, j_=j_: e.tensor_scalar(Dg[g_][:, j_, :], ID_F, cw_s[:, g_, j_:j_ + 1], None, ALU.mult), r=[cst_f, cw_s], w=[Dg[g_]])
            zT = sbuf("zT", [128, 4, 1024], BF16)
            S32 = sbuf("S32", [128, 4, 128])
            Sbf = sbuf("Sbf", [128, 4, 128], BF16)
            Sf32 = sbuf("Sf32", [128, 4, 128])
            for t_ in (S32, Sf32):
                P.op("pool", lambda e, t_=t_: e.memset(t_[:], 0.0), w=[t_])
            P.op("pool", lambda e: e.memset(Sbf[:], 0.0), w=[Sbf])

            SLT = []
            for sp_ in range(2):
                d_ = {n_: sbuf("sl_" + n_, [128, 8, 4]) for n_ in
                      ("ad", "bd", "x", "ax", "e1", "sp", "g", "bet", "gcol", "egc", "egl", "ekd", "begc", "d2", "ngcol", "ghf", "glf")}
                d_["abn"] = sbuf("abn", [128, 8, 16])
                d_["rtk"] = sbuf("rtk", [128, 8])
                d_["nAs"] = sbuf("nAs", [128, 4])
                d_["dts"] = sbuf("dts", [128, 4])
                d_["ghi"] = sbuf("ghi", [128, 32], BF16)
                d_["glo"] = sbuf("glo", [128, 32], BF16)
                SLT.append(d_)
            def ct(name, dt=BF16, n=128):
                return sbuf(name, [128, 4, n], dt)
            CT = []
            for par_ in range(2):
                T_ = {}
                for n_ in ("sqk", "knT", "qnT", "kdec", "E", "EU", "Am", "Ad", "Ao", "Bd", "Bo", "nwT", "qkm", "qdT", "Gdh", "Gdl"):
                    T_[n_] = ct(n_)
                for n_ in ("rk", "t0"):
                    T_[n_] = ct(n_, F32)
                T_["egr"] = T_["t0"]
                for n_ in ("R", "Wj", "Rm"):
                    T_[n_] = ct(n_, BF16, 256)
                T_["Pl"] = [ct("Pl%d" % i) for i in range(2)]
                T_["Ql"] = [ct("Ql%d" % i) for i in range(2)]
                T_["X"] = [ct("X%d" % i) for i in range(2)]
                CT.append(T_)
            vnew = ct("vnew")
            of32, ob32 = ct("of32", F32), ct("ob32", F32)
            ofb, obb = ct("ofb"), ct("obb")
            osq, orn, zs = ct("osq"), ct("orn", F32), ct("zs")
            ot1 = of32
            gtile = ct("gtile")

            def mm4(bank, lhs, rhs, r, n=128, off=0, sig_last=True, start=True, stop=True, w=None, first_only=False):
                for h in range(4):
                    st_ = start and (h == 0 or not first_only)
                    P.op("pe", lambda e, h=h, st_=st_: e.matmul(bank[:, off + h * n: off + (h + 1) * n], lhs(h), rhs(h), start=st_, stop=stop),
                         r=r, w=[bank] if w is None else w, sig=(sig_last and h == 3))

            def v4(bank, n=128):
                return bank[:, 0:4 * n].rearrange("p (h n) -> p h n", h=4)

            tapst = sbuf("tapst", [128, 512]) if DEBUG_TAPS else None

            def tap(i, buf, ap, n=512):
                if not DEBUG_TAPS:
                    return
                dst = tapst[:, 0:n] if ap.ndim == 2 else tapst[:, 0:n].rearrange("p (h n) -> p h n", h=4)
                P.op("dve", lambda e: e.tensor_copy(dst, ap), r=[buf], w=[tapst])
                P.dma(dbg[(i // 4) * 128:(i // 4) * 128 + 128, (i % 4) * 512:(i % 4) * 512 + n], tapst[:, 0:n], r=[tapst], w=[], key="dbg")

            def slot_fns(s, sp, with_out, ob_idx=None, of_idx=None):
                f_ap = flg_s[:, s:s + 1]
                dyn = s < 6
                rev = s in (6, 7)
                S_ = SLT[sp]
                abn, rtk, nAs, dts, ghi, glo, ghf, glf = (S_[k_] for k_ in ("abn", "rtk", "nAs", "dts", "ghi", "glo", "ghf", "glf"))

                def g_load_ab():
                    stg, sqt = B["stg"], B["sqt"]
                    tl = ntiles(WG)
                    def ld(k):
                        P.dma(stg[k % 2][:, 0:WG], xg[s][k * 128:(k + 1) * 128, :], w=[stg[k % 2]], key="stg%d" % (k % 2))
                    ld(0)
                    yield
                    yield
                    for k in range(8):
                        if k + 1 < 8:
                            ld(k + 1)
                        s_ = stg[k % 2]
                        P.op("dve", lambda e, s_=s_, k=k: e.tensor_copy(xT[:, k, 0:WG], s_[:, 0:WG]), r=[s_], w=[xT])
                        yield
                        yield
                    banks = [gps() for _ in tl]
                    for k in range(8):
                        q_ = sqt[k % 2]
                        P.op("act", lambda e, q_=q_, k=k: e.activation(q_[:, 0:WG], xT[:, k, 0:WG], AF.Square), r=[xT], w=[q_])
                        for ti_, ((o, n), bk) in enumerate(zip(tl, banks)):
                            P.op("pe", lambda e, bk=bk, q_=q_, o=o, n=n, k=k: e.matmul(bk[:, 0:n], ONE_B, q_[:, o:o + n], start=(k == 0), stop=(k == 7)),
                                 r=[q_, cst_b], w=[bk], sig=(k == 7 or ti_ == len(tl) - 1))
                    for (o, n), bk in zip(tl, banks):
                        rsqrt_op(rrow[:, o:o + n], bk[:, 0:n], 1.0 / D, EPS, [bk], [rrow])
                    yield
                    P.op("dve", lambda e: e.scalar_tensor_tensor(nAs[:], nAd[:], f_ap, nA[:, 4:8], ALU.mult, ALU.add), r=[nAd, nA, flg_s], w=[nAs])
                    P.op("dve", lambda e: e.scalar_tensor_tensor(dts[:], dtd[:], f_ap, gpar_s[:, 1, 4:8], ALU.mult, ALU.add), r=[dtd, gpar_s, flg_s], w=[dts])
                    bab = gps()
                    for c in range(8):
                        for k in range(8):
                            P.op("pe", lambda e, c=c, k=k: e.matmul(bab[:, c * 16:(c + 1) * 16], xT[:, k, 2 + c * 128: 2 + (c + 1) * 128], wab[:, k, :], start=(k == 0), stop=(k == 7)),
                                 r=[xT, wab], w=[bab], sig=(k == 7))
                    for c in range(8):
                        P.op("pe", lambda e, c=c: e.matmul(bab[:, 128 + c:129 + c], rrow[0:1, 2 + c * 128: 2 + (c + 1) * 128], cst_f[0:1, 6, 0:1], start=True, stop=True),
                             r=[rrow, cst_f], w=[bab], sig=(c == 7))
                    P.op("act", lambda e: e.activation(rtk[:], bab[:, 128:136], AF.Copy), r=[bab], w=[rtk])
                    P.op("dve", lambda e: e.tensor_tensor(abn[:], bab[:, 0:128].rearrange("p (c j) -> p c j", c=8), bc_last(rtk[:], 16), ALU.mult), r=[bab, rtk], w=[abn])
                    yield
                    P.op("dve", lambda e: e.tensor_tensor(S_["ad"][:], abn[:, :, 0:4], abn[:, :, 4:8], ALU.subtract), r=[abn], w=[S_["ad"]])
                    P.op("dve", lambda e: e.scalar_tensor_tensor(S_["ad"][:], S_["ad"][:], f_ap, abn[:, :, 4:8], ALU.mult, ALU.add), r=[S_["ad"], abn, flg_s], w=[S_["ad"]])
                    P.op("dve", lambda e: e.tensor_tensor(S_["bd"][:], abn[:, :, 8:12], abn[:, :, 12:16], ALU.subtract), r=[abn], w=[S_["bd"]])
                    P.op("dve", lambda e: e.scalar_tensor_tensor(S_["bd"][:], S_["bd"][:], f_ap, abn[:, :, 12:16], ALU.mult, ALU.add), r=[S_["bd"], abn, flg_s], w=[S_["bd"]])
                    yield
                    P.op("dve", lambda e: e.tensor_tensor(S_["x"][:], S_["ad"][:], bc_mid(dts[:], 8), ALU.add), r=[S_["ad"], dts], w=[S_["x"]])
                    P.op("dve", lambda e: e.scalar_tensor_tensor(S_["ax"][:], S_["x"][:], -1.0, S_["x"][:], ALU.mult, ALU.max), r=[S_["x"]], w=[S_["ax"]])
                    yield
                    P.op("act", lambda e: e.activation(S_["e1"][:], S_["ax"][:], AF.Exp, scale=-1.0), r=[S_["ax"]], w=[S_["e1"]])
                    P.op("act", lambda e: e.activation(S_["e1"][:], S_["e1"][:], AF.Ln, bias=one_t[:, 0:1]), r=[S_["e1"]], w=[S_["e1"]])
                    P.op("act", lambda e: e.activation(S_["bet"][:], S_["bd"][:], AF.Exp, scale=-1.0), r=[S_["bd"]], w=[S_["bet"]])
                    yield
                    P.op("dve", lambda e: e.scalar_tensor_tensor(S_["sp"][:], S_["x"][:], 0.0, S_["e1"][:], ALU.max, ALU.add), r=[S_["x"], S_["e1"]], w=[S_["sp"]])
                    P.op("dve", lambda e: e.tensor_tensor(S_["g"][:], S_["sp"][:], bc_mid(nAs[:], 8), ALU.mult), r=[S_["sp"], nAs], w=[S_["g"]])
                    P.op("dve", lambda e: e.tensor_scalar(S_["bet"][:], S_["bet"][:], 1.0, None, ALU.add), r=[S_["bet"]], w=[S_["bet"]])
                    P.op("dve", lambda e: e.reciprocal(S_["bet"][:], S_["bet"][:]), r=[S_["bet"]], w=[S_["bet"]])
                    yield
                    g2 = S_["g"][:].rearrange("p c h -> p (c h)")
                    P.op("dve", lambda e: e.tensor_copy(ghi[:], g2), r=[S_["g"]], w=[ghi])
                    P.op("dve", lambda e: e.tensor_tensor(glo[:], g2, ghi[:], ALU.subtract), r=[S_["g"], ghi], w=[glo])
                    P.op("dve", lambda e: e.tensor_copy(ghf[:].rearrange("p c h -> p (c h)"), ghi[:]), r=[ghi], w=[ghf])
                    P.op("dve", lambda e: e.tensor_copy(glf[:].rearrange("p c h -> p (c h)"), glo[:]), r=[glo], w=[glf])
                    yield
                    bg = gps()
                    P.op("pe", lambda e: e.matmul(bg[:, 0:32], cst_b[:, 2, :], ghi[:], start=True, stop=False), r=[cst_b, ghi], w=[bg], sig=False)
                    P.op("pe", lambda e: e.matmul(bg[:, 0:32], cst_b[:, 2, :], glo[:], start=False, stop=True), r=[cst_b, glo], w=[bg], sig=False)
                    P.op("pe", lambda e: e.matmul(bg[:, 32:64], ONE_B, ghi[:], start=True, stop=False), r=[cst_b, ghi], w=[bg], sig=False)
                    P.op("pe", lambda e: e.matmul(bg[:, 32:64], ONE_B, glo[:], start=False, stop=True), r=[cst_b, glo], w=[bg])
                    fl = lambda n_: S_[n_][:].rearrange("p c h -> p (c h)")
                    P.op("act", lambda e: e.activation(fl("gcol"), bg[:, 0:32], AF.Copy), r=[bg], w=[S_["gcol"]])
                    P.op("act", lambda e: e.activation(fl("egc"), bg[:, 0:32], AF.Exp), r=[bg], w=[S_["egc"]])
                    P.op("act", lambda e: e.activation(fl("egl"), bg[:, 32:64], AF.Exp), r=[bg], w=[S_["egl"]])
                    P.op("dve", lambda e: e.tensor_scalar(fl("ngcol"), bg[:, 0:32], -1.0, None, ALU.mult), r=[bg], w=[S_["ngcol"]])
                    P.op("dve", lambda e: e.tensor_tensor(fl("d2"), bg[:, 32:64], fl("gcol"), ALU.subtract), r=[bg, S_["gcol"]], w=[S_["d2"]])
                    P.op("dve", lambda e: e.tensor_tensor(fl("begc"), fl("bet"), fl("egc"), ALU.mult), r=[S_["bet"], S_["egc"]], w=[S_["begc"]])
                    yield
                    P.op("act", lambda e: e.activation(fl("ekd"), fl("d2"), AF.Exp), r=[S_["d2"]], w=[S_["ekd"]])
                    yield

                def g_proj(hf):
                    c0_ = hf * 512
                    tiles = [(c0_, 512), (c0_ + 512, 4)]
                    groups = list(range(4, 12)) + (list(range(0, 4)) if with_out else [])

                    def proj(wt_buf, wt_ap, evac):
                        for (o, n) in tiles:
                            bk = gps()
                            for k in range(8):
                                P.op("pe", lambda e, bk=bk, k=k, o=o, n=n: e.matmul(bk[:, 0:n], wt_ap[:, k, :], xT[:, k, o:o + n], start=(k == 0), stop=(k == 7)),
                                     r=[wt_buf, xT], w=[bk], sig=(k == 7))
                            evac(bk, o, n)
                            yield
                    for gi, g in enumerate(groups):
                        if g >= 4:
                            wt_buf, wt_ap = wkv, wkv[:, g - 4, :, :]
                        else:
                            wt_buf = stream_w(768 + g * 128)
                            wt_ap = wt_buf[:]
                        pr = pre[gi % 2]
                        pr2 = pre[2 + gi % 2]
                        if dyn:
                            def ev_dyn(bk, o, n, pr=pr, pr2=pr2):
                                P.op("dve", lambda e: e.scalar_tensor_tensor(pr[:, o - c0_:o - c0_ + n], bk[:, 0:n], f_ap, rrow[:, o:o + n], ALU.mult, ALU.mult), r=[bk, rrow, flg_s], w=[pr])
                                P.op("dve", lambda e: e.scalar_tensor_tensor(pr2[:, o - c0_:o - c0_ + n], bk[:, 0:n], flg_s[:, 13 + s:14 + s], rrow[:, o:o + n], ALU.mult, ALU.mult), r=[bk, rrow, flg_s], w=[pr2])
                            yield from proj(wt_buf, wt_ap, ev_dyn)
                            taps = [(j, pr) for j in range(5)] + [(4 - j, pr2) for j in range(5)]
                            shifts = list(range(5)) + list(range(5))
                        else:
                            yield from proj(wt_buf, wt_ap,
                                            lambda bk, o, n, pr=pr: P.op("dve", lambda e: e.tensor_tensor(pr[:, o - c0_:o - c0_ + n], bk[:, 0:n], rrow[:, o:o + n], ALU.mult), r=[bk, rrow], w=[pr]))
                            taps = [((4 - j) if rev else j, pr) for j in range(5)]
                            shifts = list(range(5))
                        bk = gps()
                        nmm = len(taps)
                        for i_, ((dj, src), sh) in enumerate(zip(taps, shifts)):
                            P.op("pe", lambda e, bk=bk, dj=dj, src=src, sh=sh, i_=i_, nmm=nmm, g=g: e.matmul(bk[:, :], Dg[g][:, dj, :], src[:, sh: sh + 512], start=(i_ == 0), stop=(i_ == nmm - 1)),
                                 r=[Dg[g], src], w=[bk], sig=(i_ == nmm - 1))
                        P.op("act", lambda e, bk=bk, g=g: e.activation(post[g][hf][:, :], bk[:, :], AF.Silu), r=[bk], w=[post[g][hf]])
                        yield
                    if of_idx is not None:
                        for hz in range(4):
                            wt_buf = stream_w(2304 + hz * 128)
                            def ev_z(bk, o, n, hz=hz):
                                lo_, hi_ = max(o, c0_ + 2), min(o + n, c0_ + 514)
                                if hi_ <= lo_:
                                    return
                                P.op("dve", lambda e: e.tensor_tensor(zT[:, hz, lo_ - 2: hi_ - 2], bk[:, lo_ - o:hi_ - o], rrow[:, lo_:hi_], ALU.mult), r=[bk, rrow], w=[zT])
                            yield from proj(wt_buf, wt_buf[:], ev_z)

                def prep(c, par):
                    T_ = CT[par]
                    sqk, rk, knT, qnT, kdec, R = T_["sqk"], T_["rk"], T_["knT"], T_["qnT"], T_["kdec"], T_["R"]
                    Gdh, Gdl, t0, E, EU, egr = T_["Gdh"], T_["Gdl"], T_["t0"], T_["E"], T_["EU"], T_["egr"]
                    Am, Ad, Ao, Bd, Bo, Pl, Ql, X = T_["Am"], T_["Ad"], T_["Ao"], T_["Bd"], T_["Bo"], T_["Pl"], T_["Ql"], T_["X"]
                    Wj, Rm, nwT, qkm, qdT = T_["Wj"], T_["Rm"], T_["nwT"], T_["qkm"], T_["qdT"]
                    cs = slice((c % 4) * 128, (c % 4 + 1) * 128)
                    hf_ = c // 4
                    sc = lambda n_, h: S_[n_][:, c, h:h + 1]
                    def l2n(src0, dst, scale):
                        for h in range(4):
                            P.op("act", lambda e, h=h: e.activation(sqk[:, h, :], post[src0 + h][hf_][:, cs], AF.Square), r=[post[src0 + h][hf_]], w=[sqk])
                        b1 = gps()
                        P.op("pe", lambda e: e.matmul(b1[:, :], ONE_B, sqk[:].rearrange("p h n -> p (h n)"), start=True, stop=True), r=[cst_b, sqk], w=[b1])
                        rsqrt_op(rk[:].rearrange("p h n -> p (h n)"), b1[:, :], 1.0, EPS, [b1], [rk])
                        for h in range(4):
                            P.op("dve", lambda e, h=h: e.scalar_tensor_tensor(dst[:, h, :], post[src0 + h][hf_][:, cs], scale, rk[:, h, :], ALU.mult, ALU.mult), r=[post[src0 + h][hf_], rk], w=[dst])
                    l2n(4, knT, 1.0)
                    yield
                    if with_out:
                        l2n(0, qnT, 128.0 ** -0.5)
                        yield
                    b2 = gps()
                    mm4(b2, lambda h: knT[:, h, :], lambda h: ID_B, r=[knT, cst_b])
                    P.op("dve", lambda e: e.tensor_tensor(R[:, :, 0:128], v4(b2), bc_last(S_["begc"][:, c, :], 128), ALU.mult), r=[b2, S_["begc"]], w=[R])
                    P.op("dve", lambda e: e.tensor_tensor(kdec[:], v4(b2), bc_last(S_["ekd"][:, c, :], 128), ALU.mult), r=[b2, S_["ekd"]], w=[kdec])
                    yield
                    b3 = gps()
                    mm4(b3, lambda h: post[8 + h][hf_][:, cs], lambda h: ID_B, r=[post[8][hf_], post[9][hf_], post[10][hf_], post[11][hf_], cst_b])
                    P.op("dve", lambda e: e.tensor_tensor(R[:, :, 128:256], v4(b3), bc_last(S_["bet"][:, c, :], 128), ALU.mult), r=[b3, S_["bet"]], w=[R])
                    yield
                    for h in range(4):
                        P.op("dve", lambda e, h=h: e.tensor_scalar(Gdh[:, h, :], cst_b[:, 2, :], ghf[:, c, h:h + 1], None, ALU.mult), r=[cst_b, ghf], w=[Gdh])
                        P.op("pool", lambda e, h=h: e.tensor_scalar(Gdl[:, h, :], cst_b[:, 2, :], glf[:, c, h:h + 1], None, ALU.mult), r=[cst_b, glf], w=[Gdl])
                    b4 = gps()
                    P.op("pe", lambda e: e.matmul(b4[:, :], ONE_B, Gdh[:].rearrange("p h n -> p (h n)"), start=True, stop=False), r=[cst_b, Gdh], w=[b4], sig=False)
                    P.op("pe", lambda e: e.matmul(b4[:, :], ONE_B, Gdl[:].rearrange("p h n -> p (h n)"), start=False, stop=True), r=[cst_b, Gdl], w=[b4])
                    for h in range(4):
                        P.op("act", lambda e, h=h: e.activation(t0[:, h, :], b4[:, h * 128:(h + 1) * 128], AF.Abs, bias=sc("ngcol", h)), r=[b4, S_["ngcol"]], w=[t0])
                    P.op("act", lambda e: e.activation(E[:], t0[:], AF.Exp, scale=-1.0), r=[t0], w=[E])
                    if with_out:
                        P.op("act", lambda e: e.activation(egr[:].rearrange("p h n -> p (h n)"), b4[:, :], AF.Exp), r=[b4], w=[egr])
                        P.op("pool", lambda e: e.tensor_tensor(qdT[:], qnT[:], egr[:], ALU.mult), r=[qnT, egr], w=[qdT])
                        P.op("pool", lambda e: e.tensor_tensor(EU[:], E[:], bc_mid(cst_b[:, 2, :], 4), ALU.mult), r=[E, cst_b], w=[EU])
                    yield
                    b5 = gps()
                    mm4(b5, lambda h: knT[:, h, :], lambda h: knT[:, h, :], r=[knT])
                    for h in range(4):
                        P.op("dve", lambda e, h=h: e.scalar_tensor_tensor(Am[:, h, :], b5[:, h * 128:(h + 1) * 128], sc("bet", h), E[:, h, :], ALU.mult, ALU.mult), r=[b5, S_["bet"], E], w=[Am])
                    P.op("dve", lambda e: e.tensor_tensor(Ad[:], Am[:], bc_mid(cst_b[:, 4, :], 4), ALU.mult), r=[Am, cst_b], w=[Ad])
                    P.op("pool", lambda e: e.tensor_tensor(Ao[:], Am[:], bc_mid(cst_b[:, 5, :], 4), ALU.mult), r=[Am, cst_b], w=[Ao])
                    yield
                    b6 = gps()
                    mm4(b6, lambda h: Ad[:, h, :], lambda h: ID_B, r=[Ad, cst_b])
                    P.op("act", lambda e: e.activation(Bd[:].rearrange("p h n -> p (h n)"), b6[:, :], AF.Copy), r=[b6], w=[Bd])
                    yield
                    b7 = gps()
                    mm4(b7, lambda h: Ao[:, h, :], lambda h: ID_B, r=[Ao, cst_b])
                    P.op("act", lambda e: e.activation(Bo[:].rearrange("p h n -> p (h n)"), b7[:, :], AF.Copy), r=[b7], w=[Bo])
                    yield
                    if with_out:
                        b8 = gps()
                        mm4(b8, lambda h: knT[:, h, :], lambda h: qnT[:, h, :], r=[knT, qnT])
                        P.op("dve", lambda e: e.tensor_tensor(qkm[:].rearrange("p h n -> p (h n)"), b8[:, :], EU[:].rearrange("p h n -> p (h n)"), ALU.mult), r=[b8, EU], w=[qkm])
                    yield
                    P.op("dve", lambda e: e.tensor_tensor(X[0][:], bc_mid(ID_B, 4), Bd[:], ALU.subtract), r=[cst_b, Bd], w=[X[0]])
                    Pc, Qc, Xc = Ad, Bd, X[0]
                    for lv in range(1, 5):
                        Pn = Pl[lv % 2]
                        bp = gps()
                        mm4(bp, lambda h, Qc=Qc: Qc[:, h, :], lambda h, Pc=Pc: Pc[:, h, :], r=[Qc, Pc])
                        P.op("act", lambda e, Pn=Pn, bp=bp: e.activation(Pn[:].rearrange("p h n -> p (h n)"), bp[:, :], AF.Copy), r=[bp], w=[Pn])
                        if lv < 4:
                            Qn = Ql[lv % 2]
                            bq = gps()
                            mm4(bq, lambda h, Pc=Pc: Pc[:, h, :], lambda h, Qc=Qc: Qc[:, h, :], r=[Qc, Pc])
                            P.op("dve", lambda e, Qn=Qn, bq=bq: e.tensor_copy(Qn[:].rearrange("p h n -> p (h n)"), bq[:, :]), r=[bq], w=[Qn])
                            yield
                        else:
                            Qn = None
                        Xn = X[lv % 2]
                        bx = gps()
                        mm4(bx, lambda h, Pn=Pn: Pn[:, h, :], lambda h, Xc=Xc: Xc[:, h, :], r=[Pn, Xc])
                        P.op("dve", lambda e, Xn=Xn, Xc=Xc, bx=bx: e.tensor_tensor(Xn[:].rearrange("p h n -> p (h n)"), bx[:, :], Xc[:].rearrange("p h n -> p (h n)"), ALU.add), r=[bx, Xc], w=[Xn])
                        Pc, Qc, Xc = Pn, Qn, Xn
                        yield
                    T_["Xc"] = Xc
                    JB = (PS[2 * par], PS[2 * par + 1])
                    def mmj(lhsb, rhsb):
                        for h in range(4):
                            bk = JB[h // 2]
                            P.op("pe", lambda e, h=h, bk=bk: e.matmul(bk[:, (h % 2) * 256:(h % 2) * 256 + 256], lhsb[:, h, :], rhsb[:, h, :], start=True, stop=True),
                                 r=[lhsb, rhsb], w=[bk], sig=(h % 2 == 1))
                    def jv(tile_, h2):
                        return tile_[:, 2 * h2:2 * h2 + 2, :].rearrange("p h n -> p (h n)")
                    mmj(Xc, R)
                    for h2 in range(2):
                        P.op("act", lambda e, h2=h2: e.activation(jv(Wj, h2), JB[h2][:, :], AF.Copy), r=[JB[h2]], w=[Wj])
                    yield
                    for it in range(3):
                        mmj(Bo, Wj)
                        for h2 in range(2):
                            P.op("dve", lambda e, h2=h2: e.tensor_tensor(jv(Rm, h2), jv(R, h2), JB[h2][:, :], ALU.subtract), r=[R, JB[h2]], w=[Rm])
                        yield
                        mmj(Xc, Rm)
                        for h2 in range(2):
                            P.op("act", lambda e, h2=h2: e.activation(jv(Wj, h2), JB[h2][:, :], AF.Copy), r=[JB[h2]], w=[Wj])
                        yield
                    b9 = gps()
                    mm4(b9, lambda h: Wj[:, h, 0:128], lambda h: ID_B, r=[Wj, cst_b])
                    P.op("dve", lambda e: e.tensor_scalar(nwT[:].rearrange("p h n -> p (h n)"), b9[:, :], -1.0, None, ALU.mult), r=[b9], w=[nwT])
                    yield

                def scan(c, par):
                    T_ = CT[par]
                    knT, qnT, kdec, R, E, Am = T_["knT"], T_["qnT"], T_["kdec"], T_["R"], T_["E"], T_["Am"]
                    Wj, nwT, qkm, qdT, Xc = T_["Wj"], T_["nwT"], T_["qkm"], T_["qdT"], T_["Xc"]
                    sc = lambda n_, h: S_[n_][:, c, h:h + 1]
                    if of_idx is not None:
                        P.dma(obb[:].rearrange("p h n -> p (h n)"), obw[15 - (of_idx * 8 + c)], r=[obw_buf], w=[obb], key="obr")
                    P.op("dve", lambda e: e.tensor_tensor(orn[:], S32[:], bc_last(S_["egl"][:, c, :], 128), ALU.mult), r=[S32, S_["egl"]], w=[orn])
                    if with_out:
                        bo = gps()
                        mm4(bo, lambda h: qdT[:, h, :], lambda h: Sbf[:, h, :], r=[qdT, Sbf], sig_last=False, start=True, stop=False, first_only=True)
                    bv = gps()
                    mm4(bv, lambda h: ID_B, lambda h: Wj[:, h, 128:256], r=[Wj, cst_b], sig_last=False, start=True, stop=False, first_only=True)
                    mm4(bv, lambda h: nwT[:, h, :], lambda h: Sbf[:, h, :], r=[nwT, Sbf], start=False, stop=True)
                    P.op("act", lambda e: e.activation(vnew[:].rearrange("p h n -> p (h n)"), bv[:, :], AF.Copy), r=[bv], w=[vnew])
                    bs = gps()
                    mm4(bs, lambda h: kdec[:, h, :], lambda h: vnew[:, h, :], r=[kdec, vnew])
                    fl_ = lambda t_: t_[:].rearrange("p h n -> p (h n)")
                    P.op("dve", lambda e: e.tensor_tensor(fl_(Sbf), bs[:, :], fl_(orn), ALU.add), r=[bs, orn], w=[Sbf])
                    P.op("dve", lambda e: e.tensor_tensor(fl_(S32), bs[:, :], fl_(orn), ALU.add), r=[bs, orn], w=[S32])
                    if with_out:
                        mm4(bo, lambda h: qkm[:, h, :], lambda h: vnew[:, h, :], r=[qkm, vnew], start=False, stop=True)
                        if ob_idx is not None:
                            P.op("dve", lambda e: e.tensor_copy(obb[:].rearrange("p h n -> p (h n)"), bo[:, :]), r=[bo], w=[obb])
                            P.dma(obw[ob_idx * 8 + c], obb[:].rearrange("p h n -> p (h n)"), r=[obb], w=[obw_buf], key="obw")
                        else:
                            P.op("dve", lambda e: e.tensor_copy(ofb[:].rearrange("p h n -> p (h n)"), bo[:, :]), r=[bo], w=[ofb])
                    if DEBUG_TAPS and s == 8 and c in (0, 1):
                        fl4 = lambda t_: t_[:].rearrange("p h n -> p (h n)")
                        base = 16 * c
                        tap(base + 0, knT, fl4(knT)); tap(base + 1, qnT, fl4(qnT))
                        tap(base + 2, R, R[:, :, 0:128]); tap(base + 3, R, R[:, :, 128:256])
                        tap(base + 4, E, fl4(E)); tap(base + 5, Am, fl4(Am)); tap(base + 6, Xc, fl4(Xc))
                        tap(base + 7, Wj, Wj[:, :, 0:128]); tap(base + 8, Wj, Wj[:, :, 128:256])
                        tap(base + 9, vnew, fl4(vnew)); tap(base + 10, of32, fl4(of32)); tap(base + 11, S32, fl4(S32))
                        tap(base + 12, kdec, fl4(kdec)); tap(base + 13, qkm, fl4(qkm)); tap(base + 14, qdT, fl4(qdT))
                    if of_idx is not None:
                        bt = gps()
                        mm4(bt, lambda h: ofb[:, h, :], lambda h: ID_B, r=[ofb, cst_b], sig_last=False, start=True, stop=False, first_only=True)
                        mm4(bt, lambda h: obb[:, h, :], lambda h: cst_b[:, 1, :], r=[obb, cst_b], start=False, stop=True)
                        P.op("act", lambda e: e.activation(osq[:].rearrange("p h n -> p (h n)"), bt[:, :], AF.Square), r=[bt], w=[osq])
                        bn = gps()
                        P.op("pe", lambda e: e.matmul(bn[:, :], ONE_B, osq[:].rearrange("p h n -> p (h n)"), start=True, stop=True), r=[cst_b, osq], w=[bn])
                        rsqrt_op(orn[:].rearrange("p h n -> p (h n)"), bn[:, :], 1.0 / 128, EPS, [bn], [orn])
                        tcs = slice(of_idx * 1024 + c * 128, of_idx * 1024 + (c + 1) * 128)
                        P.op("act", lambda e: e.activation(zs[:], zT[:, :, c * 128:(c + 1) * 128], AF.Silu), r=[zT], w=[zs])
                        P.op("dve", lambda e: e.tensor_tensor(ot1[:].rearrange("p h n -> p (h n)"), bt[:, :], orn[:].rearrange("p h n -> p (h n)"), ALU.mult), r=[bt, orn], w=[ot1])
                        P.op("dve", lambda e: e.scalar_tensor_tensor(gtile[:], ot1[:], gnw_s[:, 0:1], zs[:], ALU.mult, ALU.mult), r=[ot1, gnw_s, zs], w=[gtile])
                        P.dma(mixs[4:8].rearrange("h p t -> p h t")[:, :, tcs], gtile[:], r=[gtile], w=[mixs_buf], key="mixw")

                return g_load_ab, g_proj, prep, scan

            obw_buf = Buf(None, "obw")

            def capture(i):
                cap = flg_s[:, 10 + i:11 + i]
                P.op("dve", lambda e: e.scalar_tensor_tensor(Sf32[:], S32[:], cap, Sf32[:], ALU.mult, ALU.add), r=[S32, flg_s, Sf32], w=[Sf32])
                P.op("dve", lambda e: e.scalar_tensor_tensor(S32[:], S32[:], cap, S32[:], ALU.mult, ALU.subtract), r=[S32, flg_s], w=[S32])
                P.op("dve", lambda e: e.tensor_scalar(S32[:], S32[:], -1.0, None, ALU.mult), r=[S32], w=[S32])
                P.op("act", lambda e: e.activation(Sbf[:], S32[:], AF.Copy), r=[S32], w=[Sbf])

            def drain(g_):
                for _ in g_:
                    pass
            specs = [(s_, False, None, None) for s_ in range(6)] + [(6, True, 0, None), (7, True, 1, None), (8, True, None, 0), (9, True, None, 1)]
            ctx = [slot_fns(sp_[0], i_ % 2, sp_[1], ob_idx=sp_[2], of_idx=sp_[3]) for i_, sp_ in enumerate(specs)]
            if RUN_GDN:
                drain(ctx[0][0]())
                drain(ctx[0][1](0))
                for idx in range(10):
                    g_load_ab, g_proj, prep, scan = ctx[idx]
                    nxt = ctx[idx + 1] if idx + 1 < 10 else None
                    if idx == 8:
                        P.op("dve", lambda e: e.tensor_copy(S32[:], Sf32[:]), r=[Sf32], w=[S32])
                        P.op("act", lambda e: e.activation(Sbf[:], S32[:], AF.Copy), r=[S32], w=[Sbf])

                    def side2(nxt=nxt):
                        if nxt is not None:
                            yield from nxt[0]()
                            yield from nxt[1](0)
                    sgs = [g_proj(1), side2()]
                    for c0 in range(0, 8, 2):
                        sg = sgs[c0 // 4]
                        if not OVERLAP and c0 % 4 == 0:
                            drain(sg)
                        gens = [prep(c0, 0), prep(c0 + 1, 1), sg]
                        alive = True
                        while alive:
                            alive = False
                            for gi_, g_ in enumerate(gens):
                                try:
                                    next(g_)
                                    assert not P.pend["pe"][0] and not P.pend["pe"][1], "dangling unsignaled PE ops at yield"
                                    if gi_ < 2:
                                        alive = True
                                except StopIteration:
                                    pass
                        scan(c0, 0)
                        scan(c0 + 1, 1)
                        if c0 % 4 == 2:
                            drain(sg)
                    if idx in (1, 3, 5):
                        capture(idx // 2)

        cur[0] = st
        P.barrier()
        if DEBUG_MIX:
            with contextlib.ExitStack() as s_dbg:
                cur[0] = s_dbg
                dtmpb = sbuf("dtmpb", [128, 2048], BF16)
                dtmp = sbuf("dtmp", [128, 2048])
                for ci in range(8):
                    P.dma(dtmpb[:], mixs[ci], r=[mixs_buf], w=[dtmpb], key="dbgr")
                    P.op("dve", lambda e: e.tensor_copy(dtmp[:], dtmpb[:]), r=[dtmpb], w=[dtmp])
                    P.dma(dbg[ci * 128:(ci + 1) * 128, :], dtmp[:], r=[dtmp], w=[], key="dbg")
            P.barrier()
        s_ffn = st.enter_context(contextlib.ExitStack())
        cur[0] = s_ffn
        alloc_wstage(2, False)
        wo_b = sbuf("wo_b", [128, 8, D], BF16)
        w1_b = sbuf("w1_b", [128, 8, DFF], BF16)
        w2_b = sbuf("w2_b", [128, 32, D], BF16)
        for oc in range(8):
            load_w(wo_b, wo_b[:, :, oc * 128:(oc + 1) * 128], w_out[:, oc, :, :], None)
        for f in range(32):
            load_w(w1_b, w1_b[:, :, f * 128:(f + 1) * 128], w1[:, f, :, :], 1)
        for f4 in range(4):
            for oc in range(8):
                load_w(w2_b, w2_b[:, f4 * 8:(f4 + 1) * 8, oc * 128:(oc + 1) * 128], w2[:, oc, f4 * 8:(f4 + 1) * 8, :], None)
        hTs = [sbuf("hT%d" % i, [128, 8, 256]) for i in range(2)]
        xres = [sbuf("xres%d" % i, [128, 256]) for i in range(2)]
        hrss = [sbuf("hrs%d" % i, [128, 256]) for i in range(2)]
        hnTs = [sbuf("hnT%d" % i, [128, 8, 256], BF16) for i in range(2)]
        arl = [sbuf("arl%d" % i, [128, 256], BF16) for i in range(2)]
        aT = [sbuf("aT%d" % i, [128, 256], BF16) for i in range(2)]
        oT = [sbuf("oT%d" % i, [128, 256]) for i in range(2)]
        out_evs = []
        mixb = [sbuf("mixb%d" % i, [128, 8, 256], BF16) for i in range(2)]
        YB = [PS[0], PS[1], PS[2], PS[3]]

        def rms_rows(src_buf, tmp_bf, dst_rs, bk):
            P.op("act", lambda e: e.activation(tmp_bf[:], src_buf[:], AF.Square), r=[src_buf], w=[tmp_bf])
            for oc in range(8):
                P.op("pe", lambda e, oc=oc: e.matmul(bk[:, 0:256], ONE_B, tmp_bf[:, oc, :], start=(oc == 0), stop=(oc == 7)), r=[cst_b, tmp_bf], w=[bk], sig=(oc == 7))
            yield
            rsqrt_op(dst_rs[:], bk[:, 0:256], 1.0 / D, EPS, [bk], [dst_rs])

        def gen_pre(tb):
            p_ = tb % 2
            hT, hnT, hrs, mixsb = hTs[p_], hnTs[p_], hrss[p_], mixb[p_]
            tc_ = slice(tb * 256, (tb + 1) * 256)
            sl_i, so = 8 + tb // 4, 2 + (tb % 4) * 256
            P.dma(mixsb[:], mixs.rearrange("c p t -> p c t")[:, :, tc_], r=[mixs_buf], w=[mixsb], key="mixr%d" % p_)
            yield
            for oc in range(8):
                xr = xres[oc % 2]
                P.dma(xr[:], xg[sl_i, oc * 128:(oc + 1) * 128, so:so + 256], w=[xr], key="xres%d" % (oc % 2))
                bk = PS[6]
                for mc in range(8):
                    P.op("pe", lambda e, bk=bk, oc=oc, mc=mc: e.matmul(bk[:, 0:256], wo_b[:, mc, oc * 128:(oc + 1) * 128], mixsb[:, mc, :], start=(mc == 0), stop=(mc == 7)),
                         r=[wo_b, mixsb], w=[bk], sig=(mc == 7))
                yield
                P.op("dve", lambda e, bk=bk, oc=oc, xr=xr: e.tensor_tensor(hT[:, oc, :], bk[:, 0:256], xr[:], ALU.add), r=[bk, xr], w=[hT])
            yield
            yield from rms_rows(hT, hnT, hrs, PS[6])
            yield
            P.op("dve", lambda e: e.tensor_tensor(hnT[:], hT[:], bc_mid(hrs[:], 8), ALU.mult), r=[hT, hrs], w=[hnT])
            yield

        def gen_post(tb):
            p_ = tb % 2
            hT, hnT, hrs = hTs[p_], hnTs[p_], hrss[p_]
            tc_ = slice(tb * 256, (tb + 1) * 256)
            yield from rms_rows(hT, hnT, hrs, PS[7])
            yield
            for oc in range(8):
                ot_ = oT[oc % 2]
                P.op("dve", lambda e, oc=oc, ot_=ot_: e.scalar_tensor_tensor(ot_[:], hT[:, oc, :], nrm_s[:, 2, oc:oc + 1], hrs[:], ALU.mult, ALU.mult), r=[hT, nrm_s, hrs], w=[ot_])
                out_evs.append(P.dma(outT[oc * 128:(oc + 1) * 128, tc_], ot_[:], r=[ot_], w=[], key="out%d" % (oc % 2)))
                if oc % 2 == 1:
                    yield

        def ffn_main(tb, side):
            p_ = tb % 2
            hT, hnT = hTs[p_], hnTs[p_]

            def w1_mm(f):
                bk = PS[4 + f % 2]
                for k in range(8):
                    P.op("pe", lambda e, bk=bk, f=f, k=k: e.matmul(bk[:, 0:256], w1_b[:, k, f * 128:(f + 1) * 128], hnT[:, k, :], start=(k == 0), stop=(k == 7)),
                         r=[w1_b, hnT], w=[bk], sig=(k == 7))
                ar, at = arl[f % 2], aT[f % 2]
                P.op("act", lambda e, bk=bk, ar=ar: e.activation(ar[:], bk[:, 0:256], AF.Relu), r=[bk], w=[ar])
                P.op("dve", lambda e, ar=ar, at=at: e.tensor_tensor(at[:], ar[:], ar[:], ALU.mult), r=[ar], w=[at])

            def w2_mm(f):
                at = aT[f % 2]
                for oc in range(8):
                    yb = YB[oc // 2]
                    P.op("pe", lambda e, yb=yb, oc=oc, f=f, at=at: e.matmul(yb[:, (oc % 2) * 256:(oc % 2) * 256 + 256], w2_b[:, f, oc * 128:(oc + 1) * 128], at[:], start=(f == 0 and oc % 2 == 0), stop=(f == 31)),
                         r=[w2_b, at], w=[yb], sig=(oc == 7))
            w1_mm(0)
            for f in range(32):
                if f + 1 < 32:
                    w1_mm(f + 1)
                w2_mm(f)
                try:
                    next(side)
                except StopIteration:
                    pass
            for oc in range(8):
                yb = YB[oc // 2]
                P.op("dve", lambda e, yb=yb, oc=oc: e.tensor_tensor(hT[:, oc, :], yb[:, (oc % 2) * 256:(oc % 2) * 256 + 256], hT[:, oc, :], ALU.add), r=[yb, hT], w=[hT])

        def drain_(g_):
            for _ in g_:
                pass

        def chain_(*gs):
            for g_ in gs:
                if g_ is not None:
                    yield from g_
        if RUN_FFN:
            drain_(gen_pre(0))
            for tb_ in range(8):
                side = chain_(gen_post(tb_ - 1) if tb_ > 0 else None, gen_pre(tb_ + 1) if tb_ + 1 < 8 else None)
                ffn_main(tb_, side)
                drain_(side)
            drain_(gen_post(7))
        P.final_wait((out_evs[-2:] if out_evs else []) + ([("d_dbg",) + tuple(P.dsem["dbg"])] if (DEBUG_MIX or DEBUG_TAPS) else []))

        with nc.Block() as block:
            @block.tensor
            def _(e):
                P.replay("pe", e)

            @block.scalar
            def _(e):
                P.replay("act", e)

            @block.vector
            def _(e):
                P.replay("dve", e)

            @block.gpsimd
            def _(e):
                P.replay("pool", e)

            @block.sync
            def _(e):
                P.replay("sp", e)
    return nc


def _t5_bucket_table():
    import math
    rel = (np.arange(384)[None, :] - 128) - np.arange(128)[:, None]
    nb, max_exact = 16, 8
    try:
        import jax
        import jax.numpy as jnp
        with jax.default_device(jax.devices("cpu")[0]):
            rel_j = jnp.asarray(rel, dtype=jnp.int32)
            base = jnp.where(rel_j > 0, nb, 0)
            n = jnp.abs(rel_j)
            log_ratio = jnp.log(jnp.maximum(n, 1).astype(jnp.float32) / max_exact) / math.log(128 / max_exact)
            large = jnp.minimum(max_exact + (log_ratio * (nb - max_exact)).astype(jnp.int32), nb - 1)
            bucket = np.asarray(base + jnp.where(n < max_exact, n, large))
    except Exception:
        base = np.where(rel > 0, nb, 0)
        n = np.abs(rel)
        lr = np.log(np.maximum(n, 1).astype(np.float32) / np.float32(max_exact)) / np.float32(math.log(128 / max_exact))
        large = np.minimum(max_exact + (lr * np.float32(nb - max_exact)).astype(np.int32), nb - 1)
        bucket = base + np.where(n < max_exact, n, large)
    return bucket, (np.abs(rel) <= 128)


_CACHE = {}


def kernel(x, norm_mix_w, w_in, rel_bias, attn_sink, conv_w, gdn_a_log, gdn_dt_bias,
           gdn_norm_w, w_out, norm_ffn_w, w_ffn_in, w_ffn_out, norm_final_w):
    f32 = np.float32
    x = np.asarray(x, f32)
    if "nc" not in _CACHE:
        _CACHE["nc"] = build_program()
    nc = _CACHE["nc"]
    rep = lambda a: np.ascontiguousarray(np.broadcast_to(np.asarray(a, f32).reshape(1, -1), (128, np.asarray(a).size)))
    pk = lambda w, kc: np.ascontiguousarray(np.asarray(w, f32).reshape(kc, 128, -1).transpose(1, 0, 2))

    def pkg(w, kc):
        w = np.asarray(w, f32)
        n = w.shape[1]
        npad = (-n) % 128
        if npad:
            w = np.concatenate([w, np.zeros((w.shape[0], npad), f32)], axis=1)
        return np.ascontiguousarray(w.reshape(kc, 128, -1, 128).transpose(1, 2, 0, 3))
    w_in_l = pk(w_in[0], 8)
    wk = w_in_l[:, :, 512:640].reshape(128, 8, 2, 64)
    wk_dup = np.ascontiguousarray(np.concatenate([wk, wk], axis=3))
    nrm = np.ascontiguousarray(np.stack([np.asarray(norm_mix_w[0], f32).reshape(8, 128).T,
                                         np.asarray(norm_ffn_w[0], f32).reshape(8, 128).T,
                                         np.asarray(norm_final_w, f32).reshape(8, 128).T], axis=1))
    p = np.arange(128)
    blk = (p[:, None] // 32) == (p[None, :] // 32)
    ls = p[None, :] < p[:, None]
    cst = np.zeros((128, 8, 128), f32)
    cst[:, 0] = np.eye(128)
    cst[:, 1] = np.eye(128)[::-1]
    cst[:, 2] = p[:, None] <= p[None, :]
    cst[:, 3] = ls
    cst[:, 4] = ls & blk
    cst[:, 5] = ls & ~blk
    cst[:, 6] = 1.0
    cst[:, 7, 0] = p < 64
    cst[:, 7, 1] = p >= 64
    bucket, inwin = _t5_bucket_table()
    rb = np.asarray(rel_bias, f32)
    band = np.where(inwin[:, :, None], rb[bucket], f32(-1e30)).astype(f32)
    biasT = np.ascontiguousarray(band.reshape(128, 3, 128, 8).transpose(2, 3, 1, 0))
    cw = np.asarray(conv_w[0], f32)
    convw = np.ascontiguousarray(cw.reshape(5, 12, 128).transpose(2, 1, 0))
    convr = np.ascontiguousarray(convw[:, :, ::-1])
    gpar = np.ascontiguousarray(np.stack([rep(gdn_a_log[0]), rep(gdn_dt_bias[0])], axis=1))
    gnw = np.ascontiguousarray(np.asarray(gdn_norm_w[0], f32).reshape(128, 1))
    sk = np.asarray(attn_sink[0], f32)
    sink = np.zeros((128, 4), f32)
    for j in range(4):
        sink[:64, j] = sk[2 * j]
        sink[64:, j] = sk[2 * j + 1]
    shared = dict(w_in=pkg(w_in[0], 8), wk_dup=wk_dup, w_out=pkg(w_out[0], 8), w1=pkg(w_ffn_in[0], 8), w2=pkg(w_ffn_out[0], 32),
                  nrm=nrm, cst=cst, biasT=biasT, convw=convw, convr=convr, gpar=gpar, gnw=gnw, sink=sink)
    in_maps = []
    for c in range(8):
        b, r = c // 4, c % 4
        lo, hi = r * 2048, (r + 1) * 2048
        xp = np.zeros((SEQ + 256, D), f32)
        xp[128:128 + SEQ] = x[b]

        def seg(t0, t1, halo, flip):
            rows = xp[t0 - halo + 128: t1 + halo + 128]
            if flip:
                rows = rows[::-1]
            return np.ascontiguousarray(rows.T)
        slots, flags = [], []
        for i in range(2 * r):
            slots.append(seg(i * 1024, (i + 1) * 1024, 2, False)); flags.append(1.0)
        for i in range(2 * (3 - r)):
            slots.append(seg(SEQ - (i + 1) * 1024, SEQ - i * 1024, 2, True)); flags.append(0.0)
        slots.append(seg(hi - 1024, hi, 2, True)); flags.append(0.0)
        slots.append(seg(lo, lo + 1024, 2, True)); flags.append(0.0)
        slots.append(seg(lo, lo + 1024, 2, False)); flags.append(1.0)
        slots.append(seg(lo + 1024, hi, 2, False)); flags.append(1.0)
        xg = np.stack(slots, axis=0)
        xa = np.stack([seg(lo + h * 1024, lo + (h + 1) * 1024, 128, False) for h in range(2)], axis=0)
        flg = np.zeros((128, 24), f32)
        flg[:, 0:10] = np.asarray(flags, f32)[None, :]
        for i in range(3):
            flg[:, 10 + i] = 1.0 if (2 * (i + 1) == 2 * r) else 0.0
        flg[:, 13:23] = 1.0 - flg[:, 0:10]
        kbias = np.zeros((128, 2, 10), f32)
        for h in range(2):
            for kb in range(10):
                t0 = lo + h * 1024 - 128 + kb * 128
                if t0 < 0 or t0 >= SEQ:
                    kbias[:, h, kb] = -1e30
        m = dict(shared)
        m.update(xg=xg, xa=xa, flg=flg, kbias=kbias)
        in_maps.append(m)
    res = run_bass_kernel_spmd(nc, in_maps, core_ids=list(range(8)))
    out = np.empty((2, SEQ, D), f32)
    for c in range(8):
        b, r = c // 4, c % 4
        out[b, r * 2048:(r + 1) * 2048, :] = np.asarray(res.results[c]["outT"], f32).T
    if DEBUG_MIX or DEBUG_TAPS:
        kernel.dbg = [np.asarray(res.results[c]["dbg"], f32) for c in range(8)]
    return out
```

```python
import contextlib
import numpy as np
import concourse.bass as bass
import concourse.mybir as mybir
from concourse.bass_utils import run_bass_kernel_spmd

F32 = mybir.dt.float32
BF16 = mybir.dt.bfloat16
ALU = mybir.AluOpType
AF = mybir.ActivationFunctionType

SEQ = 8192
D = 1024
DFF = 4096
D_IN = 2832
EPS = 1e-6
NSLOT = 10
WG = 1028
WA = 1280
SAME_ENGINE_SYNC = True
DEBUG_MIX = False
DEBUG_TAPS = False
RUN_GDN = True
RUN_FFN = True
RUN_ATT = True
GDN_SLOTS = None
GDN_STAGE = 99
OVERLAP = True


class Buf:
    __slots__ = ("ap", "w", "r", "name")

    def __init__(self, ap, name=""):
        self.ap = ap
        self.w = None
        self.r = {}
        self.name = name

    def __getitem__(self, k):
        return self.ap[k]


class Prog:
    ENG = ("pe", "act", "dve", "pool", "sp")

    def __init__(self, nc, stack):
        self.nc = nc
        self.stack = stack
        self.sem = {k: stack.enter_context(nc.semaphore("sem_" + k)) for k in self.ENG}
        self.cnt = {k: 0 for k in self.ENG}
        self.ops = {k: [] for k in self.ENG}
        self.waited = {k: {} for k in self.ENG}
        self.pend = {k: ([], []) for k in self.ENG}
        self.dsem = {}

    def _waits(self, eng, r, w, force=False):
        need = {}

        def add(ev, raw=False):
            if ev is None:
                return
            s, sem, val = ev
            if s == "c_" + eng and (not SAME_ENGINE_SYNC or eng in ("pe", "sp") or not (raw or force)):
                return
            if need.get(s, (None, 0))[1] < val:
                need[s] = (sem, val)

        for b in r:
            add(b.w, raw=True)
        for b in w:
            add(b.w)
            for ev in b.r.values():
                add(ev)
        for s, (sem, val) in need.items():
            if self.waited[eng].get(s, 0) >= val:
                continue
            self.waited[eng][s] = val
            self.ops[eng].append(("wait", sem, val))

    def _record(self, ev, r, w):
        for b in r:
            b.r[ev[0]] = ev
        for b in w:
            b.w = ev
            b.r = {}

    def op(self, eng, fn, r=(), w=(), sig=True, force=False):
        self._waits(eng, r, w, force)
        if not sig:
            self.pend[eng][0].extend(r)
            self.pend[eng][1].extend(w)
            self.ops[eng].append(("op", fn, None, 0))
            return
        self.cnt[eng] += 1
        ev = ("c_" + eng, self.sem[eng], self.cnt[eng])
        self.ops[eng].append(("op", fn, self.sem[eng], 1))
        pr, pw = self.pend[eng]
        self._record(ev, list(r) + pr, list(w) + pw)
        self.pend[eng] = ([], [])

    def dma(self, out_ap, in_ap, r=(), w=(), key="d", q="sp"):
        self._waits(q, r, w)
        if key not in self.dsem:
            self.dsem[key] = [self.stack.enter_context(self.nc.semaphore("dsem_" + key)), 0]
        ent = self.dsem[key]
        ent[1] += 16
        ev = ("d_" + key, ent[0], ent[1])
        self.ops[q].append(("op", lambda e: e.dma_start(out=out_ap, in_=in_ap), ent[0], 16))
        self._record(ev, r, w)
        return ev

    def barrier(self):
        evs = [("c_" + k, self.sem[k], self.cnt[k]) for k in self.ENG if self.cnt[k] > 0]
        evs += [("d_" + k, v[0], v[1]) for k, v in self.dsem.items()]
        for eng in self.ENG:
            assert not self.pend[eng][0] and not self.pend[eng][1]
            for s, sem, val in evs:
                if s == "c_" + eng or self.waited[eng].get(s, 0) >= val:
                    continue
                self.waited[eng][s] = val
                self.ops[eng].append(("wait", sem, val))

    def final_wait(self, evs, q="sp"):
        for s, sem, val in evs:
            self.ops[q].append(("wait", sem, val))

    def replay(self, eng, e):
        for it in self.ops[eng]:
            if it[0] == "wait":
                e.wait_ge(it[1], it[2])
            else:
                ins = it[1](e)
                if it[2] is not None:
                    ins.then_inc(it[2], it[3])


def bc_last(ap2, n):
    return ap2.unsqueeze(2).to_broadcast([ap2.shape[0], ap2.shape[1], n])


def bc_mid(ap2, n):
    return ap2.unsqueeze(1).to_broadcast([ap2.shape[0], n, ap2.shape[1]])


def build_program():
    nc = bass.Bass("TRN2", target_bir_lowering=False)
    di = lambda name, shape: nc.dram_tensor(name, shape, F32, kind="ExternalInput").ap()
    xg = di("xg", [NSLOT, D, WG])
    xa = di("xa", [2, D, WA])
    w_in = di("w_in", [128, 23, 8, 128])
    wk_dup = di("wk_dup", [128, 8, 2, 128])
    w_out = di("w_out", [128, 8, 8, 128])
    w1 = di("w1", [128, 32, 8, 128])
    w2 = di("w2", [128, 8, 32, 128])
    nrm = di("nrm", [128, 3, 8])
    cst = di("cst", [128, 8, 128])
    biasT = di("biasT", [128, 8, 3, 128])
    kbias = di("kbias", [128, 2, 10])
    flg = di("flg", [128, 24])
    convw = di("convw", [128, 12, 5])
    convr = di("convr", [128, 12, 5])
    gpar = di("gpar", [128, 2, 8])
    gnw = di("gnw", [128, 1])
    sink = di("sink", [128, 4])
    outT = nc.dram_tensor("outT", [D, 2048], F32, kind="ExternalOutput").ap()
    dbg = nc.dram_tensor("dbg", [D, 2048], F32, kind="ExternalOutput").ap() if (DEBUG_MIX or DEBUG_TAPS) else None
    obw = nc.dram_tensor("obw", [16, 128, 512], BF16).ap()
    mixs = nc.dram_tensor("mixs", [8, 128, 2048], BF16).ap()

    with contextlib.ExitStack() as st:
        P = Prog(nc, st)

        cur = [st]
        nid = [0]

        def sb(name, shape, dt=F32):
            nid[0] += 1
            t = cur[0].enter_context(nc.sbuf_tensor("%s_%d" % (name, nid[0]), shape, dt))
            return t

        def sbuf(name, shape, dt=F32):
            return Buf(sb(name, shape, dt)[:], name)

        pst = st.enter_context(nc.psum_tensor("ps", [128, 8, 512], F32))
        PS = [Buf(pst[:, i, :], "ps%d" % i) for i in range(8)]
        gp_state = [0]
        GEN = [4, 5, 6, 7]

        def gps():
            gp_state[0] = (gp_state[0] + 1) % len(GEN)
            return PS[GEN[gp_state[0]]]

        cst_f = sbuf("cst_f", [128, 8, 128])
        cst_b = sbuf("cst_b", [128, 8, 128], BF16)
        P.dma(cst_f[:], cst, w=[cst_f], key="c0")
        P.op("dve", lambda e: e.tensor_copy(cst_b[:], cst_f[:]), r=[cst_f], w=[cst_b])
        ID_F, J_F, U_F, ONE_F = cst_f[:, 0, :], cst_f[:, 1, :], cst_f[:, 2, :], cst_f[:, 6, :]
        ID_B, ONE_B = cst_b[:, 0, :], cst_b[:, 6, :]
        LS_F, LD_F, LO_F = cst_f[:, 3, :], cst_f[:, 4, :], cst_f[:, 5, :]
        nrm_s = sbuf("nrm_s", [128, 3, 8])
        flg_s = sbuf("flg_s", [128, 24])
        cw_s = sbuf("cw_s", [128, 12, 5])
        cr_s = sbuf("cr_s", [128, 12, 5])
        cdiff = sbuf("cdiff", [128, 12, 5])
        gpar_s = sbuf("gpar_s", [128, 2, 8])
        gnw_s = sbuf("gnw_s", [128, 1])
        sink_s = sbuf("sink_s", [128, 4])
        kb_s = sbuf("kb_s", [128, 2, 10])
        small = [nrm_s, flg_s, cw_s, cr_s, gpar_s, gnw_s, sink_s, kb_s]
        for i, (b_, src) in enumerate(zip(small, [nrm, flg, convw, convr, gpar, gnw, sink, kbias])):
            P.dma(b_[:], src, w=[b_], key="c%d" % (i + 1))
        P.op("dve", lambda e: e.tensor_tensor(cdiff[:], cw_s[:], cr_s[:], ALU.subtract), r=[cw_s, cr_s], w=[cdiff])
        nA = sbuf("nA", [128, 8])
        P.op("act", lambda e: e.activation(nA[:], gpar_s[:, 0, :], AF.Exp), r=[gpar_s], w=[nA])
        P.op("dve", lambda e: e.tensor_scalar(nA[:], nA[:], -1.0, None, ALU.mult), r=[nA], w=[nA])
        nAd = sbuf("nAd", [128, 4])
        dtd = sbuf("dtd", [128, 4])
        P.op("dve", lambda e: e.tensor_tensor(nAd[:], nA[:, 0:4], nA[:, 4:8], ALU.subtract), r=[nA], w=[nAd])
        P.op("dve", lambda e: e.tensor_tensor(dtd[:], gpar_s[:, 1, 0:4], gpar_s[:, 1, 4:8], ALU.subtract), r=[gpar_s], w=[dtd])
        esink = sbuf("esink", [128, 4])
        eps_t = sbuf("eps_t", [128, 1])
        P.op("pool", lambda e: e.memset(eps_t[:], EPS), w=[eps_t])
        one_t = sbuf("one_t", [128, 1])
        P.op("pool", lambda e: e.memset(one_t[:], 1.0), w=[one_t])
        P.op("act", lambda e: e.activation(esink[:], sink_s[:], AF.Exp), r=[sink_s], w=[esink])

        WB = {}

        def alloc_wstage(nst, with_tmp):
            WB["wst"] = [sbuf("wst%d" % i, [128, 8, 128]) for i in range(nst)]
            WB["i"] = 0
            if with_tmp:
                WB["wtmp"] = [sbuf("wtmp%d" % i, [128, 8, 128], BF16) for i in range(2)]
                WB["ti"] = 0

        def load_w(dst_buf, dst_ap, src_ap, nrow):
            WB["i"] = (WB["i"] + 1) % len(WB["wst"])
            s_ = WB["wst"][WB["i"]]
            P.dma(s_[:], src_ap, w=[s_], key="wst%d" % WB["i"])
            if nrow is None:
                P.op("act", lambda e: e.activation(dst_ap, s_[:], AF.Copy), r=[s_], w=[dst_buf])
            else:
                P.op("dve", lambda e: e.tensor_tensor(dst_ap, s_[:], bc_last(nrm_s[:, nrow, :], 128), ALU.mult),
                     r=[s_, nrm_s], w=[dst_buf])

        def stream_w(c0, dup=None):
            WB["ti"] ^= 1
            t_ = WB["wtmp"][WB["ti"]]
            src = w_in[:, c0 // 128, :, :] if dup is None else wk_dup[:, :, dup, :]
            load_w(t_, t_[:], src, 0)
            return t_

        B = {}
        mixs_buf = Buf(None, "mixs")

        def rsqrt_op(dst_ap, src_ap, scale, eps, r, w):
            P.op("act", lambda e: e.activation(dst_ap, src_ap, AF.Ln, bias=eps_t[:, 0:1], scale=scale), r=list(r) + [eps_t], w=w)
            P.op("act", lambda e: e.activation(dst_ap, dst_ap, AF.Exp, scale=-0.5), r=w, w=w)

        def ntiles(Wd):
            t, o = [], 0
            while o < Wd:
                n = min(512, Wd - o)
                t.append((o, n))
                o += n
            return t

        def load_slot(src, Wd):
            xT, stg, sqt, rrow = B["xT"], B["stg"], B["sqt"], B["rrow"]
            tl = ntiles(Wd)
            banks = [gps() for _ in tl]
            for k in range(8):
                s_ = stg[k % 2]
                q_ = sqt[k % 2]
                P.dma(s_[:, 0:Wd], src[k * 128:(k + 1) * 128, :], w=[s_], key="stg%d" % (k % 2))
                P.op("dve", lambda e, s_=s_, k=k: e.tensor_copy(xT[:, k, 0:Wd], s_[:, 0:Wd]), r=[s_], w=[xT])
                P.op("act", lambda e, s_=s_, q_=q_: e.activation(q_[:, 0:Wd], s_[:, 0:Wd], AF.Square), r=[s_], w=[q_])
                for ti_, ((o, n), bk) in enumerate(zip(tl, banks)):
                    P.op("pe", lambda e, bk=bk, q_=q_, o=o, n=n, k=k: e.matmul(bk[:, 0:n], ONE_B, q_[:, o:o + n], start=(k == 0), stop=(k == 7)),
                         r=[q_, cst_b], w=[bk], sig=(k == 7 or ti_ == len(tl) - 1))
            for (o, n), bk in zip(tl, banks):
                rsqrt_op(rrow[:, o:o + n], bk[:, 0:n], 1.0 / D, EPS, [bk], [rrow])

        def proj_fm(wt_buf, wt_ap, Wd, evac):
            xT = B["xT"]
            for (o, n) in ntiles(Wd):
                bk = gps()
                for k in range(8):
                    P.op("pe", lambda e, bk=bk, k=k, o=o, n=n: e.matmul(bk[:, 0:n], wt_ap[:, k, :], xT[:, k, o:o + n], start=(k == 0), stop=(k == 7)),
                         r=[wt_buf, xT], w=[bk], sig=(k == 7))
                evac(bk, o, n)


        def alloc_slot(Wd):
            B["xT"] = sbuf("xT", [128, 8, Wd], BF16)
            B["stg"] = [sbuf("stg%d" % i, [128, Wd]) for i in range(2)]
            B["sqt"] = [sbuf("sqt%d" % i, [128, Wd], BF16) for i in range(2)]
            B["rrow"] = sbuf("rrow", [128, Wd])
            return B["xT"], B["rrow"]

        with contextlib.ExitStack() as s_att:
            cur[0] = s_att
            GEN[:] = [0, 1, 2, 3, 4, 5, 6, 7]
            xTa, rrowa = alloc_slot(WA)
            attnT = sbuf("attnT", [128, 4, 2048], BF16)
            alloc_wstage(2, True)
            QT = [sbuf("QT%d" % i, [128, 2, 1024], BF16) for i in range(4)]
            KT = [sbuf("KT%d" % i, [128, WA], BF16) for i in range(2)]
            Vp = sbuf("Vp", [128, 10, 2, 2, 128], BF16)
            bT_f = sbuf("bT_f", [128, 8, 3, 128])
            bT = sbuf("bT", [128, 8, 3, 128], BF16)
            P.dma(bT_f[:], biasT, w=[bT_f], key="bT")
            P.op("pool", lambda e: e.tensor_copy(bT[:], bT_f[:]), r=[bT_f], w=[bT])
            wv_b = sbuf("wv_b", [128, 8, 128], BF16)
            load_w(wv_b, wv_b[:], w_in[:, 5, :, :], 0)
            PTs = [sbuf("PT%d" % i, [128, 2, 3, 128], BF16) for i in range(2)]
            rtk10 = sbuf("rtk10", [128, 10])
            rden = sbuf("rden", [128, 128])
            P.op("pool", lambda e: e.memset(Vp[:], 0.0), w=[Vp])
            onesp_b = sbuf("onesp", [128, 2, 128], BF16)
            onesp = onesp_b
            P.op("pool", lambda e: e.memset(onesp_b[:], 0.0), w=[onesp_b])
            P.op("pool", lambda e: e.memset(onesp_b[:, 0, 0:64], 1.0), w=[onesp_b], force=True)
            P.op("pool", lambda e: e.memset(onesp_b[:, 1, 64:128], 1.0), w=[onesp_b], force=True)
            hm0, hm1 = cst_f[:, 7, 0:1], cst_f[:, 7, 1:2]
            def att_half(hs):
                load_slot(xa[hs], WA)
                for j in range(4):
                    wt = stream_w(j * 128)
                    def ev_q(bk, o, n, j=j):
                        lo_, hi_ = max(o, 128), min(o + n, 1152)
                        if hi_ <= lo_:
                            return
                        for e_, hm in ((0, hm0), (1, hm1)):
                            P.op("dve", lambda e, e_=e_, hm=hm: e.scalar_tensor_tensor(QT[j][:, e_, lo_ - 128:hi_ - 128], bk[:, lo_ - o:hi_ - o], hm, rrowa[:, lo_:hi_], ALU.mult, ALU.mult),
                                 r=[bk, cst_f, rrowa], w=[QT[j]])
                    proj_fm(wt, wt[:], WA, ev_q)
                for kv in range(2):
                    wt = stream_w(0, dup=kv)
                    proj_fm(wt, wt[:], WA,
                            lambda bk, o, n, kv=kv: P.op("dve", lambda e: e.scalar_tensor_tensor(KT[kv][:, o:o + n], bk[:, 0:n], 0.125, rrowa[:, o:o + n], ALU.mult, ALU.mult), r=[bk, rrowa], w=[KT[kv]]))
                brt = gps()
                for kb in range(10):
                    P.op("pe", lambda e, kb=kb: e.matmul(brt[:, kb:kb + 1], rrowa[0:1, kb * 128:(kb + 1) * 128], cst_f[0:1, 6, 0:1], start=True, stop=True), r=[rrowa, cst_f], w=[brt], sig=(kb == 9))
                P.op("act", lambda e: e.activation(rtk10[:], brt[:, 0:10], AF.Copy), r=[brt], w=[rtk10])
                for kb in range(10):
                    bk = gps()
                    for k in range(8):
                        P.op("pe", lambda e, bk=bk, kb=kb, k=k: e.matmul(bk[:, 0:128], xTa[:, k, kb * 128:(kb + 1) * 128], wv_b[:, k, :], start=(k == 0), stop=(k == 7)), r=[xTa, wv_b], w=[bk], sig=(k == 7))
                    for kv in range(2):
                        for e_ in range(2):
                            P.op("dve", lambda e, bk=bk, kb=kb, kv=kv, e_=e_: e.tensor_scalar(Vp[:, kb, kv, e_, e_ * 64:(e_ + 1) * 64], bk[:, kv * 64:(kv + 1) * 64], rtk10[:, kb:kb + 1], None, ALU.mult),
                                 r=[bk, rtk10], w=[Vp])
                def att_st(qb, j):
                    kv = j // 2
                    bse = [gps(), gps()]
                    for e_ in range(2):
                        head = 2 * j + e_
                        for kbr in range(3):
                            kb = qb + kbr
                            P.op("pe", lambda e, e_=e_, kbr=kbr, kb=kb: e.matmul(bse[e_][:, kbr * 128:(kbr + 1) * 128], KT[kv][:, kb * 128:(kb + 1) * 128], QT[j][:, e_, qb * 128:(qb + 1) * 128], start=True, stop=False),
                                 r=[KT[kv], QT[j]], w=[bse[e_]], sig=False)
                            P.op("pe", lambda e, e_=e_, kbr=kbr, head=head: e.matmul(bse[e_][:, kbr * 128:(kbr + 1) * 128], ID_B, bT[:, head, kbr, :], start=False, stop=True),
                                 r=[cst_b, bT], w=[bse[e_]], sig=(kbr == 2))
                    return bse

                def att_pv(qb, j, bse, PT):
                    kv = j // 2
                    for e_ in range(2):
                        if 1 <= qb <= 6:
                            P.op("act", lambda e, e_=e_: e.activation(PT[:, e_, :, :].rearrange("p k n -> p (k n)"), bse[e_][:, 0:384], AF.Exp), r=[bse[e_]], w=[PT])
                        else:
                            for kbr in range(3):
                                kb = qb + kbr
                                P.op("act", lambda e, e_=e_, kbr=kbr, kb=kb: e.activation(PT[:, e_, kbr, :], bse[e_][:, kbr * 128:(kbr + 1) * 128], AF.Exp, bias=kb_s[:, hs, kb:kb + 1]),
                                     r=[bse[e_], kb_s], w=[PT])
                    bo = gps()
                    n_mm = 0
                    for e_ in range(2):
                        for kbr in range(3):
                            kb = qb + kbr
                            P.op("pe", lambda e, e_=e_, kbr=kbr, kb=kb, n_mm=n_mm: e.matmul(bo[:, 0:128], Vp[:, kb, kv, e_, :], PT[:, e_, kbr, :], start=(n_mm == 0), stop=(n_mm == 5)),
                                 r=[Vp, PT], w=[bo], sig=False)
                            n_mm += 1
                    n_mm = 0
                    for e_ in range(2):
                        for kbr in range(3):
                            P.op("pe", lambda e, e_=e_, kbr=kbr, n_mm=n_mm: e.matmul(bo[:, 128:256], onesp[:, e_, :], PT[:, e_, kbr, :], start=(n_mm == 0), stop=(n_mm == 5)),
                                 r=[onesp_b, PT], w=[bo], sig=(n_mm == 5))
                            n_mm += 1
                    P.op("dve", lambda e, bo=bo, j=j: e.tensor_scalar(rden[:], bo[:, 128:256], esink[:, j:j + 1], None, ALU.add), r=[bo, esink], w=[rden])
                    P.op("dve", lambda e: e.reciprocal(rden[:], rden[:]), r=[rden], w=[rden])
                    qc = slice(hs * 1024 + qb * 128, hs * 1024 + (qb + 1) * 128)
                    P.op("dve", lambda e, bo=bo, j=j, qc=qc: e.tensor_tensor(attnT[:, j, qc], bo[:, 0:128], rden[:], ALU.mult), r=[bo, rden], w=[attnT])

                blocks = [(qb_, j_) for qb_ in range(8) for j_ in range(4)]
                bse_next = att_st(*blocks[0])
                for bi_, (qb_, j_) in enumerate(blocks):
                    bse_cur = bse_next
                    if bi_ + 1 < len(blocks):
                        bse_next = att_st(*blocks[bi_ + 1])
                    att_pv(qb_, j_, bse_cur, PTs[bi_ % 2])

            if RUN_ATT:
                att_half(0)
                att_half(1)
            for j_ in range(4):
                P.dma(mixs[j_], attnT[:, j_, :], r=[attnT], w=[mixs_buf], key="mixw")
        cur[0] = st
        P.barrier()
        GEN[:] = [4, 5, 6, 7]
        gp_state[0] = 0
        with contextlib.ExitStack() as s_gdn:
            cur[0] = s_gdn
            xT, rrow = alloc_slot(WG)
            alloc_wstage(1, True)
            wkv = sbuf("wkv", [128, 8, 8, 128], BF16)
            for g in range(8):
                c0 = 768 + 512 + g * 128
                load_w(wkv, wkv[:, g, :, :], w_in[:, c0 // 128, :, :], 0)
            wab_f = sbuf("wab_f", [128, 8, 16])
            wab = sbuf("wab", [128, 8, 16], BF16)
            P.dma(wab_f[:], w_in[:, 22, :, 0:16], w=[wab_f], key="wab")
            P.op("pool", lambda e: e.tensor_tensor(wab[:], wab_f[:], bc_last(nrm_s[:, 0, :], 16), ALU.mult), r=[wab_f, nrm_s], w=[wab])
            pre = [sbuf("pre%d" % i, [128, 516], BF16) for i in range(4)]
            post = [[sbuf("post%d_%d" % (g, hf), [128, 512], BF16) for hf in range(2)] for g in range(12)]
            Dg = [sbuf("Dg%d" % g, [128, 5, 128], BF16) for g in range(12)]
            for g_ in range(12):
                for j_ in range(5):
                    P.op("dve", lambda e, g_=g_, j_=j_: e.tensor_scalar(Dg[g_][:, j_, :], ID_F, cw_s[:, g_, j_:j_ + 1], None, ALU.mult), r=[cst_f, cw_s], w=[Dg[g_]])
            zT = sbuf("zT", [128, 4, 1024], BF16)
            S32 = sbuf("S32", [128, 4, 128])
            Sbf = sbuf("Sbf", [128, 4, 128], BF16)
            Sf32 = sbuf("Sf32", [128, 4, 128])
            for t_ in (S32, Sf32):
                P.op("pool", lambda e, t_=t_: e.memset(t_[:], 0.0), w=[t_])
            P.op("pool", lambda e: e.memset(Sbf[:], 0.0), w=[Sbf])

            SLT = []
            for sp_ in range(2):
                d_ = {n_: sbuf("sl_" + n_, [128, 8, 4]) for n_ in
                      ("ad", "bd", "x", "ax", "e1", "sp", "g", "bet", "gcol", "egc", "egl", "ekd", "begc", "d2", "ngcol", "ghf", "glf")}
                d_["abn"] = sbuf("abn", [128, 8, 16])
                d_["rtk"] = sbuf("rtk", [128, 8])
                d_["nAs"] = sbuf("nAs", [128, 4])
                d_["dts"] = sbuf("dts", [128, 4])
                d_["ghi"] = sbuf("ghi", [128, 32], BF16)
                d_["glo"] = sbuf("glo", [128, 32], BF16)
                SLT.append(d_)
            def ct(name, dt=BF16, n=128):
                return sbuf(name, [128, 4, n], dt)
            CT = []
            for par_ in range(2):
                T_ = {}
                for n_ in ("sqk", "knT", "qnT", "kdec", "E", "EU", "Am", "Ad", "Ao", "Bd", "Bo", "nwT", "qkm", "qdT", "Gdh", "Gdl"):
                    T_[n_] = ct(n_)
                for n_ in ("rk", "t0"):
                    T_[n_] = ct(n_, F32)
                T_["egr"] = T_["t0"]
                for n_ in ("R", "Wj", "Rm"):
                    T_[n_] = ct(n_, BF16, 256)
                T_["Pl"] = [ct("Pl%d" % i) for i in range(2)]
                T_["Ql"] = [ct("Ql%d" % i) for i in range(2)]
                T_["X"] = [ct("X%d" % i) for i in range(2)]
                CT.append(T_)
            vnew = ct("vnew")
            of32, ob32 = ct("of32", F32), ct("ob32", F32)
            ofb, obb = ct("ofb"), ct("obb")
            osq, orn, zs = ct("osq"), ct("orn", F32), ct("zs")
            ot1 = of32
            gtile = ct("gtile")

            def mm4(bank, lhs, rhs, r, n=128, off=0, sig_last=True, start=True, stop=True, w=None, first_only=False):
                for h in range(4):
                    st_ = start and (h == 0 or not first_only)
                    P.op("pe", lambda e, h=h, st_=st_: e.matmul(bank[:, off + h * n: off + (h + 1) * n], lhs(h), rhs(h), start=st_, stop=stop),
                         r=r, w=[bank] if w is None else w, sig=(sig_last and h == 3))

            def v4(bank, n=128):
                return bank[:, 0:4 * n].rearrange("p (h n) -> p h n", h=4)

            tapst = sbuf("tapst", [128, 512]) if DEBUG_TAPS else None

            def tap(i, buf, ap, n=512):
                if not DEBUG_TAPS:
                    return
                dst = tapst[:, 0:n] if ap.ndim == 2 else tapst[:, 0:n].rearrange("p (h n) -> p h n", h=4)
                P.op("dve", lambda e: e.tensor_copy(dst, ap), r=[buf], w=[tapst])
                P.dma(dbg[(i // 4) * 128:(i // 4) * 128 + 128, (i % 4) * 512:(i % 4) * 512 + n], tapst[:, 0:n], r=[tapst], w=[], key="dbg")

            def slot_fns(s, sp, with_out, ob_idx=None, of_idx=None):
                f_ap = flg_s[:, s:s + 1]
                dyn = s < 6
                rev = s in (6, 7)
                S_ = SLT[sp]
                abn, rtk, nAs, dts, ghi, glo, ghf, glf = (S_[k_] for k_ in ("abn", "rtk", "nAs", "dts", "ghi", "glo", "ghf", "glf"))

                def g_load_ab():
                    stg, sqt = B["stg"], B["sqt"]
                    tl = ntiles(WG)
                    def ld(k):
                        P.dma(stg[k % 2][:, 0:WG], xg[s][k * 128:(k + 1) * 128, :], w=[stg[k % 2]], key="stg%d" % (k % 2))
                    ld(0)
                    yield
                    yield
                    for k in range(8):
                        if k + 1 < 8:
                            ld(k + 1)
                        s_ = stg[k % 2]
                        P.op("dve", lambda e, s_=s_, k=k: e.tensor_copy(xT[:, k, 0:WG], s_[:, 0:WG]), r=[s_], w=[xT])
                        yield
                        yield
                    banks = [gps() for _ in tl]
                    for k in range(8):
                        q_ = sqt[k % 2]
                        P.op("act", lambda e, q_=q_, k=k: e.activation(q_[:, 0:WG], xT[:, k, 0:WG], AF.Square), r=[xT], w=[q_])
                        for ti_, ((o, n), bk) in enumerate(zip(tl, banks)):
                            P.op("pe", lambda e, bk=bk, q_=q_, o=o, n=n, k=k: e.matmul(bk[:, 0:n], ONE_B, q_[:, o:o + n], start=(k == 0), stop=(k == 7)),
                                 r=[q_, cst_b], w=[bk], sig=(k == 7 or ti_ == len(tl) - 1))
                    for (o, n), bk in zip(tl, banks):
                        rsqrt_op(rrow[:, o:o + n], bk[:, 0:n], 1.0 / D, EPS, [bk], [rrow])
                    yield
                    P.op("dve", lambda e: e.scalar_tensor_tensor(nAs[:], nAd[:], f_ap, nA[:, 4:8], ALU.mult, ALU.add), r=[nAd, nA, flg_s], w=[nAs])
                    P.op("dve", lambda e: e.scalar_tensor_tensor(dts[:], dtd[:], f_ap, gpar_s[:, 1, 4:8], ALU.mult, ALU.add), r=[dtd, gpar_s, flg_s], w=[dts])
                    bab = gps()
                    for c in range(8):
                        for k in range(8):
                            P.op("pe", lambda e, c=c, k=k: e.matmul(bab[:, c * 16:(c + 1) * 16], xT[:, k, 2 + c * 128: 2 + (c + 1) * 128], wab[:, k, :], start=(k == 0), stop=(k == 7)),
                                 r=[xT, wab], w=[bab], sig=(k == 7))
                    for c in range(8):
                        P.op("pe", lambda e, c=c: e.matmul(bab[:, 128 + c:129 + c], rrow[0:1, 2 + c * 128: 2 + (c + 1) * 128], cst_f[0:1, 6, 0:1], start=True, stop=True),
                             r=[rrow, cst_f], w=[bab], sig=(c == 7))
                    P.op("act", lambda e: e.activation(rtk[:], bab[:, 128:136], AF.Copy), r=[bab], w=[rtk])
                    P.op("dve", lambda e: e.tensor_tensor(abn[:], bab[:, 0:128].rearrange("p (c j) -> p c j", c=8), bc_last(rtk[:], 16), ALU.mult), r=[bab, rtk], w=[abn])
                    yield
                    P.op("dve", lambda e: e.tensor_tensor(S_["ad"][:], abn[:, :, 0:4], abn[:, :, 4:8], ALU.subtract), r=[abn], w=[S_["ad"]])
                    P.op("dve", lambda e: e.scalar_tensor_tensor(S_["ad"][:], S_["ad"][:], f_ap, abn[:, :, 4:8], ALU.mult, ALU.add), r=[S_["ad"], abn, flg_s], w=[S_["ad"]])
                    P.op("dve", lambda e: e.tensor_tensor(S_["bd"][:], abn[:, :, 8:12], abn[:, :, 12:16], ALU.subtract), r=[abn], w=[S_["bd"]])
                    P.op("dve", lambda e: e.scalar_tensor_tensor(S_["bd"][:], S_["bd"][:], f_ap, abn[:, :, 12:16], ALU.mult, ALU.add), r=[S_["bd"], abn, flg_s], w=[S_["bd"]])
                    yield
                    P.op("dve", lambda e: e.tensor_tensor(S_["x"][:], S_["ad"][:], bc_mid(dts[:], 8), ALU.add), r=[S_["ad"], dts], w=[S_["x"]])
                    P.op("dve", lambda e: e.scalar_tensor_tensor(S_["ax"][:], S_["x"][:], -1.0, S_["x"][:], ALU.mult, ALU.max), r=[S_["x"]], w=[S_["ax"]])
                    yield
                    P.op("act", lambda e: e.activation(S_["e1"][:], S_["ax"][:], AF.Exp, scale=-1.0), r=[S_["ax"]], w=[S_["e1"]])
                    P.op("act", lambda e: e.activation(S_["e1"][:], S_["e1"][:], AF.Ln, bias=one_t[:, 0:1]), r=[S_["e1"]], w=[S_["e1"]])
                    P.op("act", lambda e: e.activation(S_["bet"][:], S_["bd"][:], AF.Exp, scale=-1.0), r=[S_["bd"]], w=[S_["bet"]])
                    yield
                    P.op("dve", lambda e: e.scalar_tensor_tensor(S_["sp"][:], S_["x"][:], 0.0, S_["e1"][:], ALU.max, ALU.add), r=[S_["x"], S_["e1"]], w=[S_["sp"]])
                    P.op("dve", lambda e: e.tensor_tensor(S_["g"][:], S_["sp"][:], bc_mid(nAs[:], 8), ALU.mult), r=[S_["sp"], nAs], w=[S_["g"]])
                    P.op("dve", lambda e: e.tensor_scalar(S_["bet"][:], S_["bet"][:], 1.0, None, ALU.add), r=[S_["bet"]], w=[S_["bet"]])
                    P.op("dve", lambda e: e.reciprocal(S_["bet"][:], S_["bet"][:]), r=[S_["bet"]], w=[S_["bet"]])
                    yield
                    g2 = S_["g"][:].rearrange("p c h -> p (c h)")
                    P.op("dve", lambda e: e.tensor_copy(ghi[:], g2), r=[S_["g"]], w=[ghi])
                    P.op("dve", lambda e: e.tensor_tensor(glo[:], g2, ghi[:], ALU.subtract), r=[S_["g"], ghi], w=[glo])
                    P.op("dve", lambda e: e.tensor_copy(ghf[:].rearrange("p c h -> p (c h)"), ghi[:]), r=[ghi], w=[ghf])
                    P.op("dve", lambda e: e.tensor_copy(glf[:].rearrange("p c h -> p (c h)"), glo[:]), r=[glo], w=[glf])
                    yield
                    bg = gps()
                    P.op("pe", lambda e: e.matmul(bg[:, 0:32], cst_b[:, 2, :], ghi[:], start=True, stop=False), r=[cst_b, ghi], w=[bg], sig=False)
                    P.op("pe", lambda e: e.matmul(bg[:, 0:32], cst_b[:, 2, :], glo[:], start=False, stop=True), r=[cst_b, glo], w=[bg], sig=False)
                    P.op("pe", lambda e: e.matmul(bg[:, 32:64], ONE_B, ghi[:], start=True, stop=False), r=[cst_b, ghi], w=[bg], sig=False)
                    P.op("pe", lambda e: e.matmul(bg[:, 32:64], ONE_B, glo[:], start=False, stop=True), r=[cst_b, glo], w=[bg])
                    fl = lambda n_: S_[n_][:].rearrange("p c h -> p (c h)")
                    P.op("act", lambda e: e.activation(fl("gcol"), bg[:, 0:32], AF.Copy), r=[bg], w=[S_["gcol"]])
                    P.op("act", lambda e: e.activation(fl("egc"), bg[:, 0:32], AF.Exp), r=[bg], w=[S_["egc"]])
                    P.op("act", lambda e: e.activation(fl("egl"), bg[:, 32:64], AF.Exp), r=[bg], w=[S_["egl"]])
                    P.op("dve", lambda e: e.tensor_scalar(fl("ngcol"), bg[:, 0:32], -1.0, None, ALU.mult), r=[bg], w=[S_["ngcol"]])
                    P.op("dve", lambda e: e.tensor_tensor(fl("d2"), bg[:, 32:64], fl("gcol"), ALU.subtract), r=[bg, S_["gcol"]], w=[S_["d2"]])
                    P.op("dve", lambda e: e.tensor_tensor(fl("begc"), fl("bet"), fl("egc"), ALU.mult), r=[S_["bet"], S_["egc"]], w=[S_["begc"]])
                    yield
                    P.op("act", lambda e: e.activation(fl("ekd"), fl("d2"), AF.Exp), r=[S_["d2"]], w=[S_["ekd"]])
                    yield

                def g_proj(hf):
                    c0_ = hf * 512
                    tiles = [(c0_, 512), (c0_ + 512, 4)]
                    groups = list(range(4, 12)) + (list(range(0, 4)) if with_out else [])

                    def proj(wt_buf, wt_ap, evac):
                        for (o, n) in tiles:
                            bk = gps()
                            for k in range(8):
                                P.op("pe", lambda e, bk=bk, k=k, o=o, n=n: e.matmul(bk[:, 0:n], wt_ap[:, k, :], xT[:, k, o:o + n], start=(k == 0), stop=(k == 7)),
                                     r=[wt_buf, xT], w=[bk], sig=(k == 7))
                            evac(bk, o, n)
                            yield
                    for gi, g in enumerate(groups):
                        if g >= 4:
                            wt_buf, wt_ap = wkv, wkv[:, g - 4, :, :]
                        else:
                            wt_buf = stream_w(768 + g * 128)
                            wt_ap = wt_buf[:]
                        pr = pre[gi % 2]
                        pr2 = pre[2 + gi % 2]
                        if dyn:
                            def ev_dyn(bk, o, n, pr=pr, pr2=pr2):
                                P.op("dve", lambda e: e.scalar_tensor_tensor(pr[:, o - c0_:o - c0_ + n], bk[:, 0:n], f_ap, rrow[:, o:o + n], ALU.mult, ALU.mult), r=[bk, rrow, flg_s], w=[pr])
                                P.op("dve", lambda e: e.scalar_tensor_tensor(pr2[:, o - c0_:o - c0_ + n], bk[:, 0:n], flg_s[:, 13 + s:14 + s], rrow[:, o:o + n], ALU.mult, ALU.mult), r=[bk, rrow, flg_s], w=[pr2])
                            yield from proj(wt_buf, wt_ap, ev_dyn)
                            taps = [(j, pr) for j in range(5)] + [(4 - j, pr2) for j in range(5)]
                            shifts = list(range(5)) + list(range(5))
                        else:
                            yield from proj(wt_buf, wt_ap,
                                            lambda bk, o, n, pr=pr: P.op("dve", lambda e: e.tensor_tensor(pr[:, o - c0_:o - c0_ + n], bk[:, 0:n], rrow[:, o:o + n], ALU.mult), r=[bk, rrow], w=[pr]))
                            taps = [((4 - j) if rev else j, pr) for j in range(5)]
                            shifts = list(range(5))
                        bk = gps()
                        nmm = len(taps)
                        for i_, ((dj, src), sh) in enumerate(zip(taps, shifts)):
                            P.op("pe", lambda e, bk=bk, dj=dj, src=src, sh=sh, i_=i_, nmm=nmm, g=g: e.matmul(bk[:, :], Dg[g][:, dj, :], src[:, sh: sh + 512], start=(i_ == 0), stop=(i_ == nmm - 1)),
                                 r=[Dg[g], src], w=[bk], sig=(i_ == nmm - 1))
                        P.op("act", lambda e, bk=bk, g=g: e.activation(post[g][hf][:, :], bk[:, :], AF.Silu), r=[bk], w=[post[g][hf]])
                        yield
                    if of_idx is not None:
                        for hz in range(4):
                            wt_buf = stream_w(2304 + hz * 128)
                            def ev_z(bk, o, n, hz=hz):
                                lo_, hi_ = max(o, c0_ + 2), min(o + n, c0_ + 514)
                                if hi_ <= lo_:
                                    return
                                P.op("dve", lambda e: e.tensor_tensor(zT[:, hz, lo_ - 2: hi_ - 2], bk[:, lo_ - o:hi_ - o], rrow[:, lo_:hi_], ALU.mult), r=[bk, rrow], w=[zT])
                            yield from proj(wt_buf, wt_buf[:], ev_z)

                def prep(c, par):
                    T_ = CT[par]
                    sqk, rk, knT, qnT, kdec, R = T_["sqk"], T_["rk"], T_["knT"], T_["qnT"], T_["kdec"], T_["R"]
                    Gdh, Gdl, t0, E, EU, egr = T_["Gdh"], T_["Gdl"], T_["t0"], T_["E"], T_["EU"], T_["egr"]
                    Am, Ad, Ao, Bd, Bo, Pl, Ql, X = T_["Am"], T_["Ad"], T_["Ao"], T_["Bd"], T_["Bo"], T_["Pl"], T_["Ql"], T_["X"]
                    Wj, Rm, nwT, qkm, qdT = T_["Wj"], T_["Rm"], T_["nwT"], T_["qkm"], T_["qdT"]
                    cs = slice((c % 4) * 128, (c % 4 + 1) * 128)
                    hf_ = c // 4
                    sc = lambda n_, h: S_[n_][:, c, h:h + 1]
                    def l2n(src0, dst, scale):
                        for h in range(4):
                            P.op("act", lambda e, h=h: e.activation(sqk[:, h, :], post[src0 + h][hf_][:, cs], AF.Square), r=[post[src0 + h][hf_]], w=[sqk])
                        b1 = gps()
                        P.op("pe", lambda e: e.matmul(b1[:, :], ONE_B, sqk[:].rearrange("p h n -> p (h n)"), start=True, stop=True), r=[cst_b, sqk], w=[b1])
                        rsqrt_op(rk[:].rearrange("p h n -> p (h n)"), b1[:, :], 1.0, EPS, [b1], [rk])
                        for h in range(4):
                            P.op("dve", lambda e, h=h: e.scalar_tensor_tensor(dst[:, h, :], post[src0 + h][hf_][:, cs], scale, rk[:, h, :], ALU.mult, ALU.mult), r=[post[src0 + h][hf_], rk], w=[dst])
                    l2n(4, knT, 1.0)
                    yield
                    if with_out:
                        l2n(0, qnT, 128.0 ** -0.5)
                        yield
                    b2 = gps()
                    mm4(b2, lambda h: knT[:, h, :], lambda h: ID_B, r=[knT, cst_b])
                    P.op("dve", lambda e: e.tensor_tensor(R[:, :, 0:128], v4(b2), bc_last(S_["begc"][:, c, :], 128), ALU.mult), r=[b2, S_["begc"]], w=[R])
                    P.op("dve", lambda e: e.tensor_tensor(kdec[:], v4(b2), bc_last(S_["ekd"][:, c, :], 128), ALU.mult), r=[b2, S_["ekd"]], w=[kdec])
                    yield
                    b3 = gps()
                    mm4(b3, lambda h: post[8 + h][hf_][:, cs], lambda h: ID_B, r=[post[8][hf_], post[9][hf_], post[10][hf_], post[11][hf_], cst_b])
                    P.op("dve", lambda e: e.tensor_tensor(R[:, :, 128:256], v4(b3), bc_last(S_["bet"][:, c, :], 128), ALU.mult), r=[b3, S_["bet"]], w=[R])
                    yield
                    for h in range(4):
                        P.op("dve", lambda e, h=h: e.tensor_scalar(Gdh[:, h, :], cst_b[:, 2, :], ghf[:, c, h:h + 1], None, ALU.mult), r=[cst_b, ghf], w=[Gdh])
                        P.op("pool", lambda e, h=h: e.tensor_scalar(Gdl[:, h, :], cst_b[:, 2, :], glf[:, c, h:h + 1], None, ALU.mult), r=[cst_b, glf], w=[Gdl])
                    b4 = gps()
                    P.op("pe", lambda e: e.matmul(b4[:, :], ONE_B, Gdh[:].rearrange("p h n -> p (h n)"), start=True, stop=False), r=[cst_b, Gdh], w=[b4], sig=False)
                    P.op("pe", lambda e: e.matmul(b4[:, :], ONE_B, Gdl[:].rearrange("p h n -> p (h n)"), start=False, stop=True), r=[cst_b, Gdl], w=[b4])
                    for h in range(4):
                        P.op("act", lambda e, h=h: e.activation(t0[:, h, :], b4[:, h * 128:(h + 1) * 128], AF.Abs, bias=sc("ngcol", h)), r=[b4, S_["ngcol"]], w=[t0])
                    P.op("act", lambda e: e.activation(E[:], t0[:], AF.Exp, scale=-1.0), r=[t0], w=[E])
                    if with_out:
                        P.op("act", lambda e: e.activation(egr[:].rearrange("p h n -> p (h n)"), b4[:, :], AF.Exp), r=[b4], w=[egr])
                        P.op("pool", lambda e: e.tensor_tensor(qdT[:], qnT[:], egr[:], ALU.mult), r=[qnT, egr], w=[qdT])
                        P.op("pool", lambda e: e.tensor_tensor(EU[:], E[:], bc_mid(cst_b[:, 2, :], 4), ALU.mult), r=[E, cst_b], w=[EU])
                    yield
                    b5 = gps()
                    mm4(b5, lambda h: knT[:, h, :], lambda h: knT[:, h, :], r=[knT])
                    for h in range(4):
                        P.op("dve", lambda e, h=h: e.scalar_tensor_tensor(Am[:, h, :], b5[:, h * 128:(h + 1) * 128], sc("bet", h), E[:, h, :], ALU.mult, ALU.mult), r=[b5, S_["bet"], E], w=[Am])
                    P.op("dve", lambda e: e.tensor_tensor(Ad[:], Am[:], bc_mid(cst_b[:, 4, :], 4), ALU.mult), r=[Am, cst_b], w=[Ad])
                    P.op("pool", lambda e: e.tensor_tensor(Ao[:], Am[:], bc_mid(cst_b[:, 5, :], 4), ALU.mult), r=[Am, cst_b], w=[Ao])
                    yield
                    b6 = gps()
                    mm4(b6, lambda h: Ad[:, h, :], lambda h: ID_B, r=[Ad, cst_b])
                    P.op("act", lambda e: e.activation(Bd[:].rearrange("p h n -> p (h n)"), b6[:, :], AF.Copy), r=[b6], w=[Bd])
                    yield
                    b7 = gps()
                    mm4(b7, lambda h: Ao[:, h, :], lambda h: ID_B, r=[Ao, cst_b])
                    P.op("act", lambda e: e.activation(Bo[:].rearrange("p h n -> p (h n)"), b7[:, :], AF.Copy), r=[b7], w=[Bo])
                    yield
                    if with_out:
                        b8 = gps()
                        mm4(b8, lambda h: knT[:, h, :], lambda h: qnT[:, h, :], r=[knT, qnT])
                        P.op("dve", lambda e: e.tensor_tensor(qkm[:].rearrange("p h n -> p (h n)"), b8[:, :], EU[:].rearrange("p h n -> p (h n)"), ALU.mult), r=[b8, EU], w=[qkm])
                    yield
                    P.op("dve", lambda e: e.tensor_tensor(X[0][:], bc_mid(ID_B, 4), Bd[:], ALU.subtract), r=[cst_b, Bd], w=[X[0]])
                    Pc, Qc, Xc = Ad, Bd, X[0]
                    for lv in range(1, 5):
                        Pn = Pl[lv % 2]
                        bp = gps()
                        mm4(bp, lambda h, Qc=Qc: Qc[:, h, :], lambda h, Pc=Pc: Pc[:, h, :], r=[Qc, Pc])
                        P.op("act", lambda e, Pn=Pn, bp=bp: e.activation(Pn[:].rearrange("p h n -> p (h n)"), bp[:, :], AF.Copy), r=[bp], w=[Pn])
                        if lv < 4:
                            Qn = Ql[lv % 2]
                            bq = gps()
                            mm4(bq, lambda h, Pc=Pc: Pc[:, h, :], lambda h, Qc=Qc: Qc[:, h, :], r=[Qc, Pc])
                            P.op("dve", lambda e, Qn=Qn, bq=bq: e.tensor_copy(Qn[:].rearrange("p h n -> p (h n)"), bq[:, :]), r=[bq], w=[Qn])
                            yield
                        else:
                            Qn = None
                        Xn = X[lv % 2]
                        bx = gps()
                        mm4(bx, lambda h, Pn=Pn: Pn[:, h, :], lambda h, Xc=Xc: Xc[:, h, :], r=[Pn, Xc])
                        P.op("dve", lambda e, Xn=Xn, Xc=Xc, bx=bx: e.tensor_tensor(Xn[:].rearrange("p h n -> p (h n)"), bx[:, :], Xc[:].rearrange("p h n -> p (h n)"), ALU.add), r=[bx, Xc], w=[Xn])
                        Pc, Qc, Xc = Pn, Qn, Xn
                        yield
                    T_["Xc"] = Xc
                    JB = (PS[2 * par], PS[2 * par + 1])
                    def mmj(lhsb, rhsb):
                        for h in range(4):
                            bk = JB[h // 2]
                            P.op("pe", lambda e, h=h, bk=bk: e.matmul(bk[:, (h % 2) * 256:(h % 2) * 256 + 256], lhsb[:, h, :], rhsb[:, h, :], start=True, stop=True),
                                 r=[lhsb, rhsb], w=[bk], sig=(h % 2 == 1))
                    def jv(tile_, h2):
                        return tile_[:, 2 * h2:2 * h2 + 2, :].rearrange("p h n -> p (h n)")
                    mmj(Xc, R)
                    for h2 in range(2):
                        P.op("act", lambda e, h2=h2: e.activation(jv(Wj, h2), JB[h2][:, :], AF.Copy), r=[JB[h2]], w=[Wj])
                    yield
                    for it in range(3):
                        mmj(Bo, Wj)
                        for h2 in range(2):
                            P.op("dve", lambda e, h2=h2: e.tensor_tensor(jv(Rm, h2), jv(R, h2), JB[h2][:, :], ALU.subtract), r=[R, JB[h2]], w=[Rm])
                        yield
                        mmj(Xc, Rm)
                        for h2 in range(2):
                            P.op("act", lambda e, h2=h2: e.activation(jv(Wj, h2), JB[h2][:, :], AF.Copy), r=[JB[h2]], w=[Wj])
                        yield
                    b9 = gps()
                    mm4(b9, lambda h: Wj[:, h, 0:128], lambda h: ID_B, r=[Wj, cst_b])
                    P.op("dve", lambda e: e.tensor_scalar(nwT[:].rearrange("p h n -> p (h n)"), b9[:, :], -1.0, None, ALU.mult), r=[b9], w=[nwT])
                    yield

                def scan(c, par):
                    T_ = CT[par]
                    knT, qnT, kdec, R, E, Am = T_["knT"], T_["qnT"], T_["kdec"], T_["R"], T_["E"], T_["Am"]
                    Wj, nwT, qkm, qdT, Xc = T_["Wj"], T_["nwT"], T_["qkm"], T_["qdT"], T_["Xc"]
                    sc = lambda n_, h: S_[n_][:, c, h:h + 1]
                    if of_idx is not None:
                        P.op("act", lambda e: e.activation(zs[:], zT[:, :, c * 128:(c + 1) * 128], AF.Silu), r=[zT], w=[zs])
                        P.dma(obb[:].rearrange("p h n -> p (h n)"), obw[15 - (of_idx * 8 + c)], r=[obw_buf], w=[obb], key="obr")
                    P.op("dve", lambda e: e.tensor_tensor(orn[:], S32[:], bc_last(S_["egl"][:, c, :], 128), ALU.mult), r=[S32, S_["egl"]], w=[orn])
                    bv = gps()
                    mm4(bv, lambda h: ID_B, lambda h: Wj[:, h, 128:256], r=[Wj, cst_b], sig_last=False, start=True, stop=False, first_only=True)
                    mm4(bv, lambda h: nwT[:, h, :], lambda h: Sbf[:, h, :], r=[nwT, Sbf], start=False, stop=True)
                    P.op("act", lambda e: e.activation(vnew[:].rearrange("p h n -> p (h n)"), bv[:, :], AF.Copy), r=[bv], w=[vnew])
                    if with_out:
                        bo = gps()
                        mm4(bo, lambda h: qdT[:, h, :], lambda h: Sbf[:, h, :], r=[qdT, Sbf], sig_last=False, start=True, stop=False, first_only=True)
                        mm4(bo, lambda h: qkm[:, h, :], lambda h: vnew[:, h, :], r=[qkm, vnew], start=False, stop=True)
                        if ob_idx is not None:
                            P.op("dve", lambda e: e.tensor_copy(obb[:].rearrange("p h n -> p (h n)"), bo[:, :]), r=[bo], w=[obb])
                            P.dma(obw[ob_idx * 8 + c], obb[:].rearrange("p h n -> p (h n)"), r=[obb], w=[obw_buf], key="obw")
                        else:
                            P.op("dve", lambda e: e.tensor_copy(ofb[:].rearrange("p h n -> p (h n)"), bo[:, :]), r=[bo], w=[ofb])
                    bs = gps()
                    mm4(bs, lambda h: kdec[:, h, :], lambda h: vnew[:, h, :], r=[kdec, vnew])
                    fl_ = lambda t_: t_[:].rearrange("p h n -> p (h n)")
                    P.op("dve", lambda e: e.tensor_tensor(fl_(Sbf), bs[:, :], fl_(orn), ALU.add), r=[bs, orn], w=[Sbf])
                    P.op("dve", lambda e: e.tensor_tensor(fl_(S32), bs[:, :], fl_(orn), ALU.add), r=[bs, orn], w=[S32])
                    if DEBUG_TAPS and s == 8 and c in (0, 1):
                        fl4 = lambda t_: t_[:].rearrange("p h n -> p (h n)")
                        base = 16 * c
                        tap(base + 0, knT, fl4(knT)); tap(base + 1, qnT, fl4(qnT))
                        tap(base + 2, R, R[:, :, 0:128]); tap(base + 3, R, R[:, :, 128:256])
                        tap(base + 4, E, fl4(E)); tap(base + 5, Am, fl4(Am)); tap(base + 6, Xc, fl4(Xc))
                        tap(base + 7, Wj, Wj[:, :, 0:128]); tap(base + 8, Wj, Wj[:, :, 128:256])
                        tap(base + 9, vnew, fl4(vnew)); tap(base + 10, of32, fl4(of32)); tap(base + 11, S32, fl4(S32))
                        tap(base + 12, kdec, fl4(kdec)); tap(base + 13, qkm, fl4(qkm)); tap(base + 14, qdT, fl4(qdT))
                    if of_idx is not None:
                        bt = gps()
                        mm4(bt, lambda h: ofb[:, h, :], lambda h: ID_B, r=[ofb, cst_b], sig_last=False, start=True, stop=False, first_only=True)
                        mm4(bt, lambda h: obb[:, h, :], lambda h: cst_b[:, 1, :], r=[obb, cst_b], start=False, stop=True)
                        P.op("act", lambda e: e.activation(osq[:].rearrange("p h n -> p (h n)"), bt[:, :], AF.Square), r=[bt], w=[osq])
                        bn = gps()
                        P.op("pe", lambda e: e.matmul(bn[:, :], ONE_B, osq[:].rearrange("p h n -> p (h n)"), start=True, stop=True), r=[cst_b, osq], w=[bn])
                        rsqrt_op(orn[:].rearrange("p h n -> p (h n)"), bn[:, :], 1.0 / 128, EPS, [bn], [orn])
                        tcs = slice(of_idx * 1024 + c * 128, of_idx * 1024 + (c + 1) * 128)
                        P.op("dve", lambda e: e.tensor_tensor(ot1[:].rearrange("p h n -> p (h n)"), bt[:, :], orn[:].rearrange("p h n -> p (h n)"), ALU.mult), r=[bt, orn], w=[ot1])
                        P.op("dve", lambda e: e.scalar_tensor_tensor(gtile[:], ot1[:], gnw_s[:, 0:1], zs[:], ALU.mult, ALU.mult), r=[ot1, gnw_s, zs], w=[gtile])
                        P.dma(mixs[4:8].rearrange("h p t -> p h t")[:, :, tcs], gtile[:], r=[gtile], w=[mixs_buf], key="mixw")

                return g_load_ab, g_proj, prep, scan

            obw_buf = Buf(None, "obw")

            def capture(i):
                cap = flg_s[:, 10 + i:11 + i]
                P.op("dve", lambda e: e.scalar_tensor_tensor(Sf32[:], S32[:], cap, Sf32[:], ALU.mult, ALU.add), r=[S32, flg_s, Sf32], w=[Sf32])
                P.op("dve", lambda e: e.scalar_tensor_tensor(S32[:], S32[:], cap, S32[:], ALU.mult, ALU.subtract), r=[S32, flg_s], w=[S32])
                P.op("dve", lambda e: e.tensor_scalar(S32[:], S32[:], -1.0, None, ALU.mult), r=[S32], w=[S32])
                P.op("act", lambda e: e.activation(Sbf[:], S32[:], AF.Copy), r=[S32], w=[Sbf])

            def drain(g_):
                for _ in g_:
                    pass
            specs = [(s_, False, None, None) for s_ in range(6)] + [(6, True, 0, None), (7, True, 1, None), (8, True, None, 0), (9, True, None, 1)]
            ctx = [slot_fns(sp_[0], i_ % 2, sp_[1], ob_idx=sp_[2], of_idx=sp_[3]) for i_, sp_ in enumerate(specs)]
            if RUN_GDN:
                drain(ctx[0][0]())
                drain(ctx[0][1](0))
                for idx in range(10):
                    g_load_ab, g_proj, prep, scan = ctx[idx]
                    nxt = ctx[idx + 1] if idx + 1 < 10 else None
                    if idx == 8:
                        P.op("dve", lambda e: e.tensor_copy(S32[:], Sf32[:]), r=[Sf32], w=[S32])
                        P.op("act", lambda e: e.activation(Sbf[:], S32[:], AF.Copy), r=[S32], w=[Sbf])

                    def side2(nxt=nxt):
                        if nxt is not None:
                            yield from nxt[0]()
                            yield from nxt[1](0)
                    sgs = [g_proj(1), side2()]
                    for c0 in range(0, 8, 2):
                        sg = sgs[c0 // 4]
                        if not OVERLAP and c0 % 4 == 0:
                            drain(sg)
                        gens = [prep(c0, 0), prep(c0 + 1, 1), sg]
                        alive = True
                        while alive:
                            alive = False
                            for gi_, g_ in enumerate(gens):
                                try:
                                    next(g_)
                                    assert not P.pend["pe"][0] and not P.pend["pe"][1], "dangling unsignaled PE ops at yield"
                                    if gi_ < 2:
                                        alive = True
                                except StopIteration:
                                    pass
                        scan(c0, 0)
                        scan(c0 + 1, 1)
                        if c0 % 4 == 2:
                            drain(sg)
                    if idx in (1, 3, 5):
                        capture(idx // 2)

        cur[0] = st
        P.barrier()
        if DEBUG_MIX:
            with contextlib.ExitStack() as s_dbg:
                cur[0] = s_dbg
                dtmpb = sbuf("dtmpb", [128, 2048], BF16)
                dtmp = sbuf("dtmp", [128, 2048])
                for ci in range(8):
                    P.dma(dtmpb[:], mixs[ci], r=[mixs_buf], w=[dtmpb], key="dbgr")
                    P.op("dve", lambda e: e.tensor_copy(dtmp[:], dtmpb[:]), r=[dtmpb], w=[dtmp])
                    P.dma(dbg[ci * 128:(ci + 1) * 128, :], dtmp[:], r=[dtmp], w=[], key="dbg")
            P.barrier()
        s_ffn = st.enter_context(contextlib.ExitStack())
        cur[0] = s_ffn
        alloc_wstage(2, False)
        wo_b = sbuf("wo_b", [128, 8, D], BF16)
        w1_b = sbuf("w1_b", [128, 8, DFF], BF16)
        w2_b = sbuf("w2_b", [128, 32, D], BF16)
        for oc in range(8):
            load_w(wo_b, wo_b[:, :, oc * 128:(oc + 1) * 128], w_out[:, oc, :, :], None)
        for f in range(32):
            load_w(w1_b, w1_b[:, :, f * 128:(f + 1) * 128], w1[:, f, :, :], 1)
        for f4 in range(4):
            for oc in range(8):
                load_w(w2_b, w2_b[:, f4 * 8:(f4 + 1) * 8, oc * 128:(oc + 1) * 128], w2[:, oc, f4 * 8:(f4 + 1) * 8, :], None)
        hTs = [sbuf("hT%d" % i, [128, 8, 256]) for i in range(2)]
        xres = [sbuf("xres%d" % i, [128, 256]) for i in range(2)]
        hrss = [sbuf("hrs%d" % i, [128, 256]) for i in range(2)]
        hnTs = [sbuf("hnT%d" % i, [128, 8, 256], BF16) for i in range(2)]
        arl = [sbuf("arl%d" % i, [128, 256], BF16) for i in range(2)]
        aT = [sbuf("aT%d" % i, [128, 256], BF16) for i in range(2)]
        oT = [sbuf("oT%d" % i, [128, 256]) for i in range(2)]
        out_evs = []
        mixb = [sbuf("mixb%d" % i, [128, 8, 256], BF16) for i in range(2)]
        YB = [PS[0], PS[1], PS[2], PS[3]]

        def rms_rows(src_buf, tmp_bf, dst_rs, bk):
            P.op("act", lambda e: e.activation(tmp_bf[:], src_buf[:], AF.Square), r=[src_buf], w=[tmp_bf])
            for oc in range(8):
                P.op("pe", lambda e, oc=oc: e.matmul(bk[:, 0:256], ONE_B, tmp_bf[:, oc, :], start=(oc == 0), stop=(oc == 7)), r=[cst_b, tmp_bf], w=[bk], sig=(oc == 7))
            yield
            rsqrt_op(dst_rs[:], bk[:, 0:256], 1.0 / D, EPS, [bk], [dst_rs])

        def gen_pre(tb):
            p_ = tb % 2
            hT, hnT, hrs, mixsb = hTs[p_], hnTs[p_], hrss[p_], mixb[p_]
            tc_ = slice(tb * 256, (tb + 1) * 256)
            sl_i, so = 8 + tb // 4, 2 + (tb % 4) * 256
            P.dma(mixsb[:], mixs.rearrange("c p t -> p c t")[:, :, tc_], r=[mixs_buf], w=[mixsb], key="mixr%d" % p_)
            yield
            for oc in range(8):
                xr = xres[oc % 2]
                P.dma(xr[:], xg[sl_i, oc * 128:(oc + 1) * 128, so:so + 256], w=[xr], key="xres%d" % (oc % 2))
                bk = PS[6]
                for mc in range(8):
                    P.op("pe", lambda e, bk=bk, oc=oc, mc=mc: e.matmul(bk[:, 0:256], wo_b[:, mc, oc * 128:(oc + 1) * 128], mixsb[:, mc, :], start=(mc == 0), stop=(mc == 7)),
                         r=[wo_b, mixsb], w=[bk], sig=(mc == 7))
                yield
                P.op("dve", lambda e, bk=bk, oc=oc, xr=xr: e.tensor_tensor(hT[:, oc, :], bk[:, 0:256], xr[:], ALU.add), r=[bk, xr], w=[hT])
            yield
            yield from rms_rows(hT, hnT, hrs, PS[6])
            yield
            P.op("dve", lambda e: e.tensor_tensor(hnT[:], hT[:], bc_mid(hrs[:], 8), ALU.mult), r=[hT, hrs], w=[hnT])
            yield

        def gen_post(tb):
            p_ = tb % 2
            hT, hnT, hrs = hTs[p_], hnTs[p_], hrss[p_]
            tc_ = slice(tb * 256, (tb + 1) * 256)
            yield from rms_rows(hT, hnT, hrs, PS[7])
            yield
            for oc in range(8):
                ot_ = oT[oc % 2]
                P.op("dve", lambda e, oc=oc, ot_=ot_: e.scalar_tensor_tensor(ot_[:], hT[:, oc, :], nrm_s[:, 2, oc:oc + 1], hrs[:], ALU.mult, ALU.mult), r=[hT, nrm_s, hrs], w=[ot_])
                out_evs.append(P.dma(outT[oc * 128:(oc + 1) * 128, tc_], ot_[:], r=[ot_], w=[], key="out%d" % (oc % 2)))
                if oc % 2 == 1:
                    yield

        def ffn_main(tb, side):
            p_ = tb % 2
            hT, hnT = hTs[p_], hnTs[p_]

            def w1_mm(f):
                bk = PS[4 + f % 2]
                for k in range(8):
                    P.op("pe", lambda e, bk=bk, f=f, k=k: e.matmul(bk[:, 0:256], w1_b[:, k, f * 128:(f + 1) * 128], hnT[:, k, :], start=(k == 0), stop=(k == 7)),
                         r=[w1_b, hnT], w=[bk], sig=(k == 7))
                ar, at = arl[f % 2], aT[f % 2]
                P.op("act", lambda e, bk=bk, ar=ar: e.activation(ar[:], bk[:, 0:256], AF.Relu), r=[bk], w=[ar])
                P.op("dve", lambda e, ar=ar, at=at: e.tensor_tensor(at[:], ar[:], ar[:], ALU.mult), r=[ar], w=[at])

            def w2_mm(f):
                at = aT[f % 2]
                for oc in range(8):
                    yb = YB[oc // 2]
                    P.op("pe", lambda e, yb=yb, oc=oc, f=f, at=at: e.matmul(yb[:, (oc % 2) * 256:(oc % 2) * 256 + 256], w2_b[:, f, oc * 128:(oc + 1) * 128], at[:], start=(f == 0 and oc % 2 == 0), stop=(f == 31)),
                         r=[w2_b, at], w=[yb], sig=(oc == 7))
            w1_mm(0)
            for f in range(32):
                if f + 1 < 32:
                    w1_mm(f + 1)
                w2_mm(f)
                try:
                    next(side)
                except StopIteration:
                    pass
            for oc in range(8):
                yb = YB[oc // 2]
                P.op("dve", lambda e, yb=yb, oc=oc: e.tensor_tensor(hT[:, oc, :], yb[:, (oc % 2) * 256:(oc % 2) * 256 + 256], hT[:, oc, :], ALU.add), r=[yb, hT], w=[hT])

        def drain_(g_):
            for _ in g_:
                pass

        def chain_(*gs):
            for g_ in gs:
                if g_ is not None:
                    yield from g_
        if RUN_FFN:
            drain_(gen_pre(0))
            for tb_ in range(8):
                side = chain_(gen_post(tb_ - 1) if tb_ > 0 else None, gen_pre(tb_ + 1) if tb_ + 1 < 8 else None)
                ffn_main(tb_, side)
                drain_(side)
            drain_(gen_post(7))
        P.final_wait((out_evs[-2:] if out_evs else []) + ([("d_dbg",) + tuple(P.dsem["dbg"])] if (DEBUG_MIX or DEBUG_TAPS) else []))

        with nc.Block() as block:
            @block.tensor
            def _(e):
                P.replay("pe", e)

            @block.scalar
            def _(e):
                P.replay("act", e)

            @block.vector
            def _(e):
                P.replay("dve", e)

            @block.gpsimd
            def _(e):
                P.replay("pool", e)

            @block.sync
            def _(e):
                P.replay("sp", e)
    return nc


def _t5_bucket_table():
    import math
    rel = (np.arange(384)[None, :] - 128) - np.arange(128)[:, None]
    nb, max_exact = 16, 8
    try:
        import jax
        import jax.numpy as jnp
        with jax.default_device(jax.devices("cpu")[0]):
            rel_j = jnp.asarray(rel, dtype=jnp.int32)
            base = jnp.where(rel_j > 0, nb, 0)
            n = jnp.abs(rel_j)
            log_ratio = jnp.log(jnp.maximum(n, 1).astype(jnp.float32) / max_exact) / math.log(128 / max_exact)
            large = jnp.minimum(max_exact + (log_ratio * (nb - max_exact)).astype(jnp.int32), nb - 1)
            bucket = np.asarray(base + jnp.where(n < max_exact, n, large))
    except Exception:
        base = np.where(rel > 0, nb, 0)
        n = np.abs(rel)
        lr = np.log(np.maximum(n, 1).astype(np.float32) / np.float32(max_exact)) / np.float32(math.log(128 / max_exact))
        large = np.minimum(max_exact + (lr * np.float32(nb - max_exact)).astype(np.int32), nb - 1)
        bucket = base + np.where(n < max_exact, n, large)
    return bucket, (np.abs(rel) <= 128)


_CACHE = {}


def kernel(x, norm_mix_w, w_in, rel_bias, attn_sink, conv_w, gdn_a_log, gdn_dt_bias,
           gdn_norm_w, w_out, norm_ffn_w, w_ffn_in, w_ffn_out, norm_final_w):
    f32 = np.float32
    x = np.asarray(x, f32)
    if "nc" not in _CACHE:
        _CACHE["nc"] = build_program()
    nc = _CACHE["nc"]
    rep = lambda a: np.ascontiguousarray(np.broadcast_to(np.asarray(a, f32).reshape(1, -1), (128, np.asarray(a).size)))
    pk = lambda w, kc: np.ascontiguousarray(np.asarray(w, f32).reshape(kc, 128, -1).transpose(1, 0, 2))

    def pkg(w, kc):
        w = np.asarray(w, f32)
        n = w.shape[1]
        npad = (-n) % 128
        if npad:
            w = np.concatenate([w, np.zeros((w.shape[0], npad), f32)], axis=1)
        return np.ascontiguousarray(w.reshape(kc, 128, -1, 128).transpose(1, 2, 0, 3))
    w_in_l = pk(w_in[0], 8)
    wk = w_in_l[:, :, 512:640].reshape(128, 8, 2, 64)
    wk_dup = np.ascontiguousarray(np.concatenate([wk, wk], axis=3))
    nrm = np.ascontiguousarray(np.stack([np.asarray(norm_mix_w[0], f32).reshape(8, 128).T,
                                         np.asarray(norm_ffn_w[0], f32).reshape(8, 128).T,
                                         np.asarray(norm_final_w, f32).reshape(8, 128).T], axis=1))
    p = np.arange(128)
    blk = (p[:, None] // 32) == (p[None, :] // 32)
    ls = p[None, :] < p[:, None]
    cst = np.zeros((128, 8, 128), f32)
    cst[:, 0] = np.eye(128)
    cst[:, 1] = np.eye(128)[::-1]
    cst[:, 2] = p[:, None] <= p[None, :]
    cst[:, 3] = ls
    cst[:, 4] = ls & blk
    cst[:, 5] = ls & ~blk
    cst[:, 6] = 1.0
    cst[:, 7, 0] = p < 64
    cst[:, 7, 1] = p >= 64
    bucket, inwin = _t5_bucket_table()
    rb = np.asarray(rel_bias, f32)
    band = np.where(inwin[:, :, None], rb[bucket], f32(-1e30)).astype(f32)
    biasT = np.ascontiguousarray(band.reshape(128, 3, 128, 8).transpose(2, 3, 1, 0))
    cw = np.asarray(conv_w[0], f32)
    convw = np.ascontiguousarray(cw.reshape(5, 12, 128).transpose(2, 1, 0))
    convr = np.ascontiguousarray(convw[:, :, ::-1])
    gpar = np.ascontiguousarray(np.stack([rep(gdn_a_log[0]), rep(gdn_dt_bias[0])], axis=1))
    gnw = np.ascontiguousarray(np.asarray(gdn_norm_w[0], f32).reshape(128, 1))
    sk = np.asarray(attn_sink[0], f32)
    sink = np.zeros((128, 4), f32)
    for j in range(4):
        sink[:64, j] = sk[2 * j]
        sink[64:, j] = sk[2 * j + 1]
    shared = dict(w_in=pkg(w_in[0], 8), wk_dup=wk_dup, w_out=pkg(w_out[0], 8), w1=pkg(w_ffn_in[0], 8), w2=pkg(w_ffn_out[0], 32),
                  nrm=nrm, cst=cst, biasT=biasT, convw=convw, convr=convr, gpar=gpar, gnw=gnw, sink=sink)
    in_maps = []
    for c in range(8):
        b, r = c // 4, c % 4
        lo, hi = r * 2048, (r + 1) * 2048
        xp = np.zeros((SEQ + 256, D), f32)
        xp[128:128 + SEQ] = x[b]

        def seg(t0, t1, halo, flip):
            rows = xp[t0 - halo + 128: t1 + halo + 128]
            if flip:
                rows = rows[::-1]
            return np.ascontiguousarray(rows.T)
        slots, flags = [], []
        for i in range(2 * r):
            slots.append(seg(i * 1024, (i + 1) * 1024, 2, False)); flags.append(1.0)
        for i in range(2 * (3 - r)):
            slots.append(seg(SEQ - (i + 1) * 1024, SEQ - i * 1024, 2, True)); flags.append(0.0)
        slots.append(seg(hi - 1024, hi, 2, True)); flags.append(0.0)
        slots.append(seg(lo, lo + 1024, 2, True)); flags.append(0.0)
        slots.append(seg(lo, lo + 1024, 2, False)); flags.append(1.0)
        slots.append(seg(lo + 1024, hi, 2, False)); flags.append(1.0)
        xg = np.stack(slots, axis=0)
        xa = np.stack([seg(lo + h * 1024, lo + (h + 1) * 1024, 128, False) for h in range(2)], axis=0)
        flg = np.zeros((128, 24), f32)
        flg[:, 0:10] = np.asarray(flags, f32)[None, :]
        for i in range(3):
            flg[:, 10 + i] = 1.0 if (2 * (i + 1) == 2 * r) else 0.0
        flg[:, 13:23] = 1.0 - flg[:, 0:10]
        kbias = np.zeros((128, 2, 10), f32)
        for h in range(2):
            for kb in range(10):
                t0 = lo + h * 1024 - 128 + kb * 128
                if t0 < 0 or t0 >= SEQ:
                    kbias[:, h, kb] = -1e30
        m = dict(shared)
        m.update(xg=xg, xa=xa, flg=flg, kbias=kbias)
        in_maps.append(m)
    res = run_bass_kernel_spmd(nc, in_maps, core_ids=list(range(8)))
    out = np.empty((2, SEQ, D), f32)
    for c in range(8):
        b, r = c // 4, c % 4
        out[b, r * 2048:(r + 1) * 2048, :] = np.asarray(res.results[c]["outT"], f32).T
    if DEBUG_MIX or DEBUG_TAPS:
        kernel.dbg = [np.asarray(res.results[c]["dbg"], f32) for c in range(8)]
    return out
```
